# Optimizing a Trainium2 kernel written in Bass

```python
import math
import jax, jax.numpy as jnp
from jax import lax
import numpy as np

D_MODEL = 1024
BATCH = 2
SEQ = 8192
DEPTH = 4

N_MIXERS = 2
BLK = 128
ROPE_THETA = 500000.0
EPS = 1e-6
NEG_INF = -1e30

DILATED_GROUPS = ((128, 1), (512, 4), (2048, 16))
A_N_GROUPS = len(DILATED_GROUPS)
A_HEAD_DIM = 128
A_HEADS = D_MODEL // A_HEAD_DIM
A_WIDTH = A_HEADS * A_HEAD_DIM
A_ROT = A_HEAD_DIM // 4
A_QKV = 3 * A_N_GROUPS * A_WIDTH
A_IN = A_QKV + A_WIDTH

B_HEADS = 8
B_HEAD_DIM = 64
B_WIDTH = B_HEADS * 2 * B_HEAD_DIM
B_ROT = B_HEAD_DIM // 4
B_IN = 4 * B_WIDTH

N_A_LAYERS = (DEPTH + 1) // 2
N_B_LAYERS = DEPTH // 2

kernel_name = "hybrid_dilated_diff_gated_trunk"


def rms_norm(x, g):
    xf = x.astype(jnp.float32)
    y = xf * lax.rsqrt(jnp.mean(xf * xf, axis=-1, keepdims=True) + EPS)
    return (y * g.astype(jnp.float32)).astype(x.dtype)


def rope_tables(seq, rot_dim):
    inv = 1.0 / (ROPE_THETA ** (jnp.arange(0, rot_dim, 2, dtype=jnp.float32) / rot_dim))
    ang = jnp.arange(seq, dtype=jnp.float32)[:, None] * inv[None, :]
    return jnp.cos(ang), jnp.sin(ang)


def apply_partial_rope(x, cos, sin):
    half = cos.shape[-1]
    x1, x2, xp = x[..., :half], x[..., half:2 * half], x[..., 2 * half:]
    c = cos[None, :, None, :].astype(x.dtype)
    s = sin[None, :, None, :].astype(x.dtype)
    return jnp.concatenate([x1 * c - x2 * s, x2 * c + x1 * s, xp], axis=-1)


def shift_blocks(t, j):
    nb = t.shape[2]
    return jnp.pad(t, ((0, 0), (0, 0), (j, 0), (0, 0), (0, 0), (0, 0)))[:, :, :nb]


def dilated_group_attention(q, k, v, window, dilation):
    b, s, h, d = q.shape
    r = dilation
    w_sub = window // dilation
    n_prev = -(-w_sub // BLK)
    seg = r * BLK
    s_pad = -(-s // seg) * seg
    L = s_pad // r
    nb = L // BLK

    def to_blocks(t):
        t = jnp.pad(t, ((0, 0), (0, s_pad - s), (0, 0), (0, 0)))
        t = t.reshape(b, L, r, h, d).transpose(0, 2, 1, 3, 4)
        return t.reshape(b, r, nb, BLK, h, d)

    qb, kb, vb = to_blocks(q), to_blocks(k), to_blocks(v)
    kc = jnp.concatenate([shift_blocks(kb, j) for j in range(n_prev, 0, -1)] + [kb], axis=3)
    vc = jnp.concatenate([shift_blocks(vb, j) for j in range(n_prev, 0, -1)] + [vb], axis=3)

    scores = jnp.einsum('brnqhd,brnkhd->brnhqk', qb, kc,
                        preferred_element_type=jnp.float32) * (d ** -0.5)
    n_keys = (n_prev + 1) * BLK
    qi = jnp.arange(BLK)[:, None]
    ki = jnp.arange(n_keys)[None, :]
    dist = qi + n_prev * BLK - ki
    band = (dist >= 0) & (dist <= w_sub)
    kvalid = (jnp.arange(nb)[:, None] * BLK - n_prev * BLK + ki) >= 0
    mask = band[None, :, :] & kvalid[:, None, :]
    scores = jnp.where(mask[None, None, :, None, :, :], scores, NEG_INF)
    lse = jax.nn.logsumexp(scores, axis=-1)
    p = jnp.exp(scores - lse[..., None]).astype(v.dtype)
    o = jnp.einsum('brnhqk,brnkhd->brnqhd', p, vc)

    o = o.reshape(b, r, L, h, d).transpose(0, 2, 1, 3, 4).reshape(b, s_pad, h, d)[:, :s]
    lse = lse.transpose(0, 1, 2, 4, 3).reshape(b, r, L, h).transpose(0, 2, 1, 3)
    lse = lse.reshape(b, s_pad, h)[:, :s]
    return o, lse


def dilated_mixer(hn, w_in, w_out, cos, sin):
    b, s, _ = hn.shape
    proj = hn @ w_in
    qkv = proj[..., :A_QKV].reshape(b, s, A_N_GROUPS, 3, A_HEADS, A_HEAD_DIM)
    z = proj[..., A_QKV:]
    outs, lses = [], []
    for g, (window, dilation) in enumerate(DILATED_GROUPS):
        q = apply_partial_rope(qkv[:, :, g, 0], cos, sin)
        k = apply_partial_rope(qkv[:, :, g, 1], cos, sin)
        v = qkv[:, :, g, 2]
        o, l = dilated_group_attention(q, k, v, window, dilation)
        outs.append(o)
        lses.append(l)
    alpha = jax.nn.softmax(jnp.stack(lses, axis=0), axis=0)
    o = jnp.einsum('gbsh,gbshd->bshd', alpha.astype(outs[0].dtype), jnp.stack(outs, axis=0))
    y = o.reshape(b, s, A_WIDTH) * jax.nn.silu(z)
    return y @ w_out


def diff_mixer(hn, w_in, lam_params, subln, w_out, cos, sin, lam_init):
    b, s, _ = hn.shape
    proj = hn @ w_in
    q = proj[..., :B_WIDTH].reshape(b, s, B_HEADS, 2, B_HEAD_DIM)
    k = proj[..., B_WIDTH:2 * B_WIDTH].reshape(b, s, B_HEADS, 2, B_HEAD_DIM)
    v = proj[..., 2 * B_WIDTH:3 * B_WIDTH].reshape(b, s, B_HEADS, 2 * B_HEAD_DIM)
    z = proj[..., 3 * B_WIDTH:]
    q1 = apply_partial_rope(q[..., 0, :], cos, sin)
    q2 = apply_partial_rope(q[..., 1, :], cos, sin)
    k1 = apply_partial_rope(k[..., 0, :], cos, sin)
    k2 = apply_partial_rope(k[..., 1, :], cos, sin)

    lp = lam_params.astype(jnp.float32)
    lam = jnp.exp(jnp.sum(lp[0] * lp[1])) - jnp.exp(jnp.sum(lp[2] * lp[3])) + lam_init

    nb = s // BLK
    qb1 = q1.reshape(b, nb, BLK, B_HEADS, B_HEAD_DIM).transpose(1, 0, 2, 3, 4)
    qb2 = q2.reshape(b, nb, BLK, B_HEADS, B_HEAD_DIM).transpose(1, 0, 2, 3, 4)
    kpos = jnp.arange(s)
    scale = B_HEAD_DIM ** -0.5

    def block(args):
        i, a1, a2 = args
        qpos = i * BLK + jnp.arange(BLK)
        causal = (kpos[None, :] <= qpos[:, None])[None, None]
        s1 = jnp.einsum('bqhd,bkhd->bhqk', a1, k1, preferred_element_type=jnp.float32) * scale
        s2 = jnp.einsum('bqhd,bkhd->bhqk', a2, k2, preferred_element_type=jnp.float32) * scale
        p1 = jax.nn.softmax(jnp.where(causal, s1, NEG_INF), axis=-1)
        p2 = jax.nn.softmax(jnp.where(causal, s2, NEG_INF), axis=-1)
        a = (p1 - lam * p2).astype(v.dtype)
        return jnp.einsum('bhqk,bkhe->bqhe', a, v)

    o = lax.map(block, (jnp.arange(nb), qb1, qb2))
    o = o.transpose(1, 0, 2, 3, 4).reshape(b, s, B_HEADS, 2 * B_HEAD_DIM)
    o = rms_norm(o, subln) * (1.0 - lam_init)
    y = o.reshape(b, s, B_WIDTH) * jax.nn.silu(z)
    return y @ w_out


def setup_inputs(seed: int = 0) -> dict:
    key = jax.random.key(seed)
    ks = jax.random.split(key, 10)
    f32 = jnp.float32
    x = jax.random.normal(ks[0], (BATCH, SEQ, D_MODEL), f32)
    a_norm = 1.0 + 0.02 * jax.random.normal(ks[1], (N_A_LAYERS, D_MODEL), f32)
    a_w_in = jax.random.normal(ks[2], (N_A_LAYERS, D_MODEL, A_IN), f32) * D_MODEL ** -0.5
    a_w_out = jax.random.normal(ks[3], (N_A_LAYERS, A_WIDTH, D_MODEL), f32) * A_WIDTH ** -0.5
    b_norm = 1.0 + 0.02 * jax.random.normal(ks[4], (N_B_LAYERS, D_MODEL), f32)
    b_w_in = jax.random.normal(ks[5], (N_B_LAYERS, D_MODEL, B_IN), f32) * D_MODEL ** -0.5
    b_lambda = 0.1 * jax.random.normal(ks[6], (N_B_LAYERS, 4, B_HEAD_DIM), f32)
    b_subln = 1.0 + 0.02 * jax.random.normal(ks[7], (N_B_LAYERS, 2 * B_HEAD_DIM), f32)
    b_w_out = jax.random.normal(ks[8], (N_B_LAYERS, B_WIDTH, D_MODEL), f32) * B_WIDTH ** -0.5
    final_norm = 1.0 + 0.02 * jax.random.normal(ks[9], (D_MODEL,), f32)
    return {"x": x, "a_norm": a_norm, "a_w_in": a_w_in, "a_w_out": a_w_out,
            "b_norm": b_norm, "b_w_in": b_w_in, "b_lambda": b_lambda,
            "b_subln": b_subln, "b_w_out": b_w_out, "final_norm": final_norm}


def reference(x, a_norm, a_w_in, a_w_out, b_norm, b_w_in, b_lambda, b_subln, b_w_out, final_norm):
    s = x.shape[1]
    cos_a, sin_a = rope_tables(s, A_ROT)
    cos_b, sin_b = rope_tables(s, B_ROT)
    for i in range(DEPTH):
        j = i // N_MIXERS
        if i % N_MIXERS == 0:
            x = x + dilated_mixer(rms_norm(x, a_norm[j]), a_w_in[j], a_w_out[j], cos_a, sin_a)
        else:
            lam_init = 0.8 - 0.6 * math.exp(-0.3 * i)
            x = x + diff_mixer(rms_norm(x, b_norm[j]), b_w_in[j], b_lambda[j], b_subln[j],
                               b_w_out[j], cos_b, sin_b, lam_init)
    return rms_norm(x, final_norm)
```

```python
import contextlib
import numpy as np
import ml_dtypes
import concourse.bass as bass
import concourse.mybir as mybir
from concourse.bass_utils import run_bass_kernel_spmd

F32 = mybir.dt.float32
BF16 = mybir.dt.bfloat16
AF = mybir.ActivationFunctionType
ALU = mybir.AluOpType
AX = mybir.AxisListType


class _Op:
    __slots__ = ("eng", "fn", "reads", "writes", "dma", "signal", "sigval", "sem", "waits", "barrier")

    def __init__(self, eng, fn, reads, writes, dma=False, barrier=False):
        self.eng = eng
        self.fn = fn
        self.reads = reads
        self.writes = writes
        self.dma = dma
        self.signal = False
        self.sigval = 0
        self.sem = None
        self.waits = []
        self.barrier = barrier


class Prog:
    ENGS = ("pe", "act", "dve", "pool", "sp")

    def __init__(self, nc, n_dma_sems=6):
        self.nc = nc
        self.ops = []
        self.stack = contextlib.ExitStack()
        self.n_dma_sems = n_dma_sems
        self._uid = 0

    def sb(self, name, shape, dtype):
        return self.stack.enter_context(self.nc.sbuf_tensor(name, list(shape), dtype))

    def ps(self, name, shape, dtype):
        return self.stack.enter_context(self.nc.psum_tensor(name, list(shape), dtype))

    def op(self, eng, fn, reads=(), writes=()):
        self.ops.append(_Op(eng, fn, tuple(reads), tuple(writes)))

    def dma(self, q, out, in_, reads=(), writes=(), **kw):
        self.ops.append(_Op(q, lambda e: e.dma_start(out=out, in_=in_, **kw), tuple(reads), tuple(writes), dma=True))

    def barrier(self):
        self.ops.append(_Op(None, None, (), (), barrier=True))

    def finish(self):
        nc = self.nc
        self.barrier()
        ops = self.ops
        last_writer = {}
        readers = {}
        deps = [None] * len(ops)
        for j, o in enumerate(ops):
            if o.barrier:
                continue
            d = set()
            xr = tuple(t for t in o.reads if isinstance(t, tuple) and t[0] == "ps")
            if xr:
                o.reads = tuple(t for t in o.reads if t not in xr)
                o.writes = tuple(o.writes) + tuple(t for t in xr if t not in o.writes)
            for t in o.reads:
                w = last_writer.get(t)
                if w is not None:
                    d.add(w)
            for t in o.writes:
                w = last_writer.get(t)
                if w is not None:
                    d.add(w)
                for r in readers.get(t, ()):
                    d.add(r)
            for t in o.writes:
                last_writer[t] = j
                readers[t] = []
            for t in o.reads:
                readers.setdefault(t, []).append(j)
            d.discard(j)
            keep = []
            for i in d:
                p = ops[i]
                if not p.dma and not o.dma and p.eng == o.eng:
                    if o.eng in ("pe", "sp"):
                        continue
                keep.append(i)
            best = {}
            kk = []
            for i in keep:
                p = ops[i]
                if p.dma:
                    kk.append(i)
                elif i > best.get(p.eng, -1):
                    best[p.eng] = i
            keep = kk + list(best.values())
            deps[j] = keep
            for i in keep:
                ops[i].signal = True
        last_on = {}
        for j, o in enumerate(ops):
            if o.barrier:
                for e, i in last_on.items():
                    ops[i].signal = True
            elif not o.dma:
                last_on[o.eng] = j
        sems = {}
        for e in self.ENGS:
            sems[e] = self.stack.enter_context(nc.semaphore("s_" + e))
        dma_sems = {}
        for q in ("sp", "act", "pool"):
            dma_sems[q] = [self.stack.enter_context(nc.semaphore("d_%s%d" % (q, k))) for k in range(self.n_dma_sems)]
        cnt = {e: 0 for e in self.ENGS}
        dcnt = {q: 0 for q in dma_sems}
        dtot = {}
        prev_dma_wait = {}
        for j, o in enumerate(ops):
            if o.barrier:
                continue
            if o.dma:
                k = dcnt[o.eng] % self.n_dma_sems
                dcnt[o.eng] += 1
                s = dma_sems[o.eng][k]
                prev = dtot.get((o.eng, k), 0)
                if prev:
                    prev_dma_wait[j] = (s, prev)
                dtot[(o.eng, k)] = prev + 16
                o.sem = s
                o.sigval = prev + 16
            elif o.signal:
                cnt[o.eng] += 1
                o.sem = sems[o.eng]
                o.sigval = cnt[o.eng]
        known = {e: {} for e in self.ENGS}
        cur_sig = {e: 0 for e in self.ENGS}
        cur_dma = {}
        per_eng = {e: [] for e in self.ENGS}
        for j, o in enumerate(ops):
            if o.barrier:
                for e in self.ENGS:
                    w = []
                    for e2 in self.ENGS:
                        if e2 != e and cur_sig[e2] > known[e].get(id(sems[e2]), 0):
                            w.append((sems[e2], cur_sig[e2]))
                            known[e][id(sems[e2])] = cur_sig[e2]
                    for (s, v) in cur_dma.values():
                        if v > known[e].get(id(s), 0):
                            w.append((s, v))
                            known[e][id(s)] = v
                    if w:
                        per_eng[e].append((w, None))
                continue
            need = {}
            for i in deps[j]:
                p = ops[i]
                key = id(p.sem)
                if p.sigval > need.get(key, (None, 0))[1]:
                    need[key] = (p.sem, p.sigval)
            if j in prev_dma_wait:
                s, v = prev_dma_wait[j]
                if v > need.get(id(s), (None, 0))[1]:
                    need[id(s)] = (s, v)
            w = []
            for key, (s, v) in need.items():
                if v > known[o.eng].get(key, 0):
                    w.append((s, v))
                    known[o.eng][key] = v
            per_eng[o.eng].append((w, o))
            if o.dma:
                cur_dma[id(o.sem)] = (o.sem, o.sigval)
            elif o.signal:
                cur_sig[o.eng] = o.sigval
        engobj = {"pe": "tensor", "act": "scalar", "dve": "vector", "pool": "gpsimd", "sp": "sync"}
        block = self.stack.enter_context(nc.Block())

        def make(e):
            def body(eng):
                for w, o in per_eng[e]:
                    for (s, v) in w:
                        eng.wait_ge(s, v)
                    if o is None:
                        continue
                    ins = o.fn(eng)
                    if o.dma:
                        ins.then_inc(o.sem, 16)
                    elif o.signal:
                        ins.then_inc(o.sem, 1)
            return body

        for e in self.ENGS:
            getattr(block, engobj[e])(make(e))
        self.n_ops = len(ops)
        self.stack.close()


NCORES = 8
NT = 2048
NTILE = 16
DM = 1024
SEQ = 8192
EPS = 1e-6
ROPE_THETA = 500000.0
DIL = (1, 4, 16)
A_IN = 10240
B_IN = 4096


DBG = {}


class Arena:
    def __init__(self, P, nbytes):
        self.t = P.sb("arena", [128, nbytes // 2], BF16)
        self.n = nbytes // 2
        self.off = 0

    def mark(self):
        return self.off

    def reset(self, m):
        self.off = m

    def alloc(self, shape, dtype):
        esz = 4 if dtype == F32 else 2
        n = 1
        for s in shape[1:]:
            n *= s
        nb16 = n * esz // 2
        self.off = (self.off + 15) // 16 * 16
        assert self.off + nb16 <= self.n, "arena overflow %d + %d > %d" % (self.off, nb16, self.n)
        ap = self.t[0:shape[0], self.off:self.off + nb16]
        self.off += nb16
        if dtype == F32:
            ap = ap.bitcast(F32)
        if len(shape) == 3:
            ap = ap.rearrange("p (a b) -> p a b", b=shape[2])
        elif len(shape) == 4:
            ap = ap.rearrange("p (a b c) -> p a b c", b=shape[2], c=shape[3])
        return ap


class Ctx:
    def __init__(self, nc, need_x=True):
        self.nc = nc
        self.P = Prog(nc)
        self.ar = Arena(self.P, 204 * 1024)
        self.ps = self.P.ps("ps", [128, 8, 512], F32)
        self.X = self.ar.alloc([128, NTILE, DM], F32) if need_x else None
        self.ident = self.ar.alloc([128, 128], BF16)
        self.base = self.ar.mark()
        self.uid = 0
        self.dq = 0

    def q(self):
        return "sp"

    def q2(self):
        self.dq += 1
        return ("sp", "act")[self.dq % 2]

    def stage_begin(self):
        self.P.barrier()
        self.ar.reset(self.base)

    def dram(self, name, shape, dtype, kind):
        return self.nc.dram_tensor(name, list(shape), dtype, kind=kind)


def emit_consts(c, ident_d):
    c.P.dma("sp", c.ident, ident_d.ap(), writes=["ident"])


def emit_load_x(c, x_d):
    xv = x_d.ap().rearrange("(t p) f -> p t f", p=128)
    for k in range(4):
        c.P.dma(c.q(), c.X[:, 4 * k:4 * k + 4, :], xv[:, 4 * k:4 * k + 4, :],
                writes=[("X", t) for t in range(4 * k, 4 * k + 4)])


def emit_rstd(c, ss, rs, rstd, n, denom, reads):
    P = c.P
    P.op("dve", lambda e: e.tensor_scalar(rs, ss, 1.0 / denom, EPS, ALU.mult, ALU.add), reads=reads, writes=["rs"])
    P.op("act", lambda e: e.activation(rs, rs, AF.Sqrt), reads=["rs"], writes=["rs2"])
    P.op("dve", lambda e: e.reciprocal(rstd, rs), reads=["rs2"], writes=["rstd"])


def emit_proj(c, kind, w_d, g_d, rope_d, send_d, zs_d):
    P, ar, ps = c.P, c.ar, c.ps
    c.stage_begin()
    ncols = A_IN if kind == "A" else B_IN
    nblk = ncols // 512
    if kind == "A":
        nh, hd, half = 4, 128, 16
    else:
        nh, hd, half = 8, 64, 8
    hnT = ar.alloc([128, 8, NT], BF16)
    sq = ar.alloc([128, 2, DM], BF16)
    hn = ar.alloc([128, 2, DM], BF16)
    ss = ar.alloc([128, NTILE], F32)
    rs = ar.alloc([128, NTILE], F32)
    rstd = ar.alloc([128, NTILE], F32)
    G = ar.alloc([128, 8], F32)
    ngr = 3 if kind == "A" else 1
    rope = ar.alloc([128, ngr * NTILE * 2, half], F32)
    wst = ar.alloc([128, 2 * 8, 512], F32)
    wb = ar.alloc([128, 2 * 8, 512], BF16)
    st = ar.alloc([128, 4, 512], BF16)
    tmp = ar.alloc([128, 4 * 4, nh * half], F32)
    X, ident = c.X, c.ident

    P.dma("sp", G, g_d.ap(), writes=["G"])
    P.dma("sp", rope, rope_d.ap(), writes=["rope"])
    for t in range(NTILE):
        P.op("act", lambda e, t=t: e.activation(sq[:, t % 2, :], X[:, t, :], AF.Square, accum_out=ss[:, t:t + 1]),
             reads=[("X", t)], writes=[("ss", t), ("sq", t % 2)])
    lvl = DBG.get("lvl", 9)
    if lvl >= 2:
        emit_rstd(c, ss, rs, rstd, NTILE, DM, [("ss", t) for t in range(NTILE)])
    for t in range(NTILE if lvl >= 3 else 0):
        P.op("act", lambda e, t=t: e.activation(hn[:, t % 2, :], X[:, t, :], AF.Copy, scale=rstd[:, t:t + 1]),
             reads=[("X", t), "rstd"], writes=[("hn", t % 2)])
        if lvl < 4:
            continue
        bank = t % 2
        tp = ps[:, bank, :].bitcast(BF16)
        for cc in range(8):
            P.op("pe", lambda e, t=t, cc=cc, tp=tp: e.transpose(tp[:, cc * 128:(cc + 1) * 128],
                                                             hn[:, t % 2, cc * 128:(cc + 1) * 128], ident),
                 reads=[("hn", t % 2), "ident"], writes=[("ps", bank)])
        if lvl < 5:
            continue
        P.op("dve", lambda e, t=t, tp=tp: e.tensor_copy(hnT[:, :, t * 128:(t + 1) * 128],
                                                        tp.rearrange("p (c k) -> p c k", k=128)),
             reads=[("ps", bank)], writes=["hnT"])

    wv = w_d.ap().rearrange("(c p) n -> p c n", p=128)
    cnt = 0
    def wload(blk):
        slot = blk % 2
        for h2 in range(2):
            P.dma("sp", wst[:, slot * 8 + h2 * 4:slot * 8 + h2 * 4 + 4, :],
                  wv[:, h2 * 4:h2 * 4 + 4, blk * 512:(blk + 1) * 512], writes=[("wst", slot, h2)])

    blks = list(DBG.get("blks", range(nblk)))
    wload(blks[0])
    for bi, blk in enumerate(blks):
        slot = blk % 2
        if bi + 1 < len(blks):
            wload(blks[bi + 1])
        for cc in range(8):
            eng = "act" if cc % 2 == 0 else "pool"
            if eng == "act":
                P.op("act", lambda e, cc=cc, slot=slot: e.activation(wb[:, slot * 8 + cc, :], wst[:, slot * 8 + cc, :],
                                                                   AF.Copy, scale=G[:, cc:cc + 1]),
                     reads=[("wst", slot, cc // 4), "G"], writes=[("wb", slot, cc)])
            else:
                P.op("pool", lambda e, cc=cc, slot=slot: e.tensor_scalar(wb[:, slot * 8 + cc, :], wst[:, slot * 8 + cc, :],
                                                                       G[:, cc:cc + 1], None, ALU.mult),
                     reads=[("wst", slot, cc // 4), "G"], writes=[("wb", slot, cc)])
        if kind == "A":
            sec = blk // 2
            hh = blk % 2
            if sec < 9:
                g, which = sec // 3, sec % 3
            else:
                g, which = 0, 3
        else:
            which = blk // 2
            hh = blk % 2
            g = 0
        r = DIL[g] if kind == "A" else 1
        for tt in range(NTILE):
            bank = 2 + (cnt % 6)
            k4 = cnt % 4
            cnt += 1
            per = NTILE // r
            rr, nl = tt // per, tt % per
            start = rr + r * 128 * nl
            stop = start + r * 127 + 1
            for cc in range(8):
                P.op("pe", lambda e, cc=cc, slot=slot, bank=bank, start=start, stop=stop, r=r:
                     e.matmul(ps[:, bank, :], hnT[:, cc, start:stop:r], wb[:, slot * 8 + cc, :],
                              start=(cc == 0), stop=(cc == 7)),
                     reads=["hnT", ("wb", slot, cc)], writes=[("ps", bank)])
            pv = ps[:, bank, :].rearrange("p (h d) -> p h d", d=hd)
            sv = st[:, k4, :].rearrange("p (h d) -> p h d", d=hd)
            if which in (0, 1):
                ri = (g * NTILE + tt) * 2
                if DBG.get("nobc"):
                    C = rope[:, 0:nh, :]
                    S = rope[:, 0:nh, :]
                else:
                    C = rope[:, ri, :].unsqueeze(1).to_broadcast([128, nh, half])
                    S = rope[:, ri + 1, :].unsqueeze(1).to_broadcast([128, nh, half])
                tv = [tmp[:, k4 * 4 + i, :].rearrange("p (h d) -> p h d", d=half) for i in range(4)]
                x1, x2 = pv[:, :, 0:half], pv[:, :, half:2 * half]
                rl = DBG.get("rope_lvl", 9)
                P.op("act", lambda e, pv=pv, sv=sv: e.activation(sv[:, :, 2 * half:], pv[:, :, 2 * half:], AF.Copy),
                     reads=[("ps", bank)], writes=[("st", k4, "p")])
                ser = [("st", k4, "p")] if DBG.get("ser") else []
                if rl >= 1:
                  P.op("dve", lambda e, tv=tv, x1=x1, C=C: e.tensor_tensor(tv[0], x1, C, ALU.mult),
                     reads=[("ps", bank), "rope"] + ser, writes=[("tmp", k4, 0)])
                if rl >= 2:
                  P.op("dve", lambda e, tv=tv, x2=x2, S=S: e.tensor_tensor(tv[1], x2, S, ALU.mult),
                     reads=[("ps", bank), "rope"], writes=[("tmp", k4, 1)])
                if rl >= 3:
                  P.op("dve", lambda e, tv=tv, x2=x2, C=C: e.tensor_tensor(tv[2], x2, C, ALU.mult),
                     reads=[("ps", bank), "rope"], writes=[("tmp", k4, 2)])
                if rl >= 3:
                  P.op("dve", lambda e, tv=tv, x1=x1, S=S: e.tensor_tensor(tv[3], x1, S, ALU.mult),
                     reads=[("ps", bank), "rope"], writes=[("tmp", k4, 3)])
                if rl >= 4:
                  P.op(DBG.get("rope_eng", "dve"), lambda e, tv=tv, sv=sv: e.tensor_tensor(sv[:, :, 0:half], tv[0], tv[1], ALU.subtract),
                     reads=[("tmp", k4, 0), ("tmp", k4, 1)], writes=[("st", k4, "r1")])
                if rl >= 5:
                  P.op(DBG.get("rope_eng", "dve"), lambda e, tv=tv, sv=sv: e.tensor_tensor(sv[:, :, half:2 * half], tv[2], tv[3], ALU.add),
                     reads=[("tmp", k4, 2), ("tmp", k4, 3)], writes=[("st", k4, "r2")])
                streads = [("st", k4, "r1"), ("st", k4, "r2"), ("st", k4, "p")]
            elif which == 2:
                streads = [("st", k4, "r1"), ("st", k4, "r2"), ("st", k4, "p")]
                P.op("act", lambda e, k4=k4, bank=bank: e.activation(st[:, k4, :], ps[:, bank, :], AF.Copy),
                     reads=[("ps", bank)], writes=streads)
            else:
                streads = [("st", k4, "r1"), ("st", k4, "r2"), ("st", k4, "p")]
                P.op("act", lambda e, k4=k4, bank=bank: e.activation(st[:, k4, :], ps[:, bank, :], AF.Silu),
                     reads=[("ps", bank)], writes=streads)
            if which < 3:
                slotidx = g * 3 + which if kind == "A" else which
                dst = send_d.ap()[slotidx, tt * 128:(tt + 1) * 128, hh * 512:(hh + 1) * 512]
                P.dma(c.q2(), dst, st[:, k4, :], reads=streads, writes=[("send", slotidx, hh, tt)])
            else:
                P.dma(c.q2(), zs_d.ap()[tt * 128:(tt + 1) * 128, hh * 512:(hh + 1) * 512], st[:, k4, :],
                      reads=streads, writes=[("zs", hh, tt)])


def _load_tok(c, dst, src_of_quarter, name):
    for qi in range(4):
        c.P.dma(c.q(), dst[:, qi * 16:(qi + 1) * 16, :], src_of_quarter(qi), writes=[(name, qi)])


def _transpose_tok(c, src, dstT, name, nameT, bank0):
    P, ps = c.P, c.ps
    for grp in range(8):
        bank = bank0 + grp % 2
        tp = ps[:, bank, :].bitcast(BF16)
        for k in range(8):
            blk = grp * 8 + k
            P.op("pe", lambda e, tp=tp, k=k, blk=blk: e.transpose(tp[:, k * 128:(k + 1) * 128], src[:, blk, 0:128], c.ident),
                 reads=[(name, blk // 16), "ident"], writes=[("ps", bank)])
        if grp % 2 == 0:
            P.op("dve", lambda e, tp=tp, grp=grp: e.tensor_copy(dstT[:, grp * 1024:(grp + 1) * 1024], tp),
                 reads=[("ps", bank)], writes=[(nameT, grp)])
        else:
            P.op("act", lambda e, tp=tp, grp=grp: e.activation(dstT[:, grp * 1024:(grp + 1) * 1024], tp, AF.Copy),
                 reads=[("ps", bank)], writes=[(nameT, grp)])


def _copy_v(c, Vtok, Vaug, vname):
    for hf in range(2):
        c.P.op("pool", lambda e, hf=hf: e.tensor_copy(Vaug[:, hf * 32:(hf + 1) * 32, 0:128], Vtok[:, hf * 32:(hf + 1) * 32, :]),
               reads=[(vname, 2 * hf), (vname, 2 * hf + 1)], writes=[("V", hf)])


def emit_attn_A(c, tok_src, ret_dst, oscr_d, mask_d, nb=2):
    P, ar, ps = c.P, c.ar, c.ps
    c.stage_begin()
    toks = [[ar.alloc([128, 64, 128], BF16) for _ in range(3)] for _ in range(2)]
    Vaug = ar.alloc([128, 64, 130], BF16)
    QT = ar.alloc([128, SEQ], BF16)
    KT = ar.alloc([128, SEQ], BF16)
    mask = ar.alloc([128, 512], BF16)
    PT = ar.alloc([128, 4, 512], BF16)
    osb = ar.alloc([128, 8, 130], F32)
    oc = ar.alloc([128, 2 * 4 * 3, 130], F32)
    osum = ar.alloc([128, 2 * 4, 130], F32)
    rec = ar.alloc([128, 2 * 4], F32)
    ob = ar.alloc([128, 2 * 4, 128], BF16)
    scale = 128.0 ** -0.5
    P.dma("sp", mask, mask_d.ap(), writes=["mask"])
    P.op("pool", lambda e: e.memset(Vaug[:, :, 128:130], 1.0), writes=["Vones"])
    n = 0
    units = [(b, g) for b in range(nb) for g in range(3)]

    def loads(u):
        b_, g_ = units[u]
        for w3 in range(3):
            _load_tok(c, toks[u % 2][w3], lambda qi, w3=w3: tok_src(b_, g_ * 3 + w3, qi), ("tok", u % 2, w3))

    loads(0)
    for u, (b, g) in enumerate(units):
        if True:
            r = DIL[g]
            per = NTILE // r
            if u + 1 < len(units):
                loads(u + 1)
            _copy_v(c, toks[u % 2][2], Vaug, ("tok", u % 2, 2))
            _transpose_tok(c, toks[u % 2][0], QT, ("tok", u % 2, 0), "QT", 0)
            _transpose_tok(c, toks[u % 2][1], KT, ("tok", u % 2, 1), "KT", 0)

            def prevblk(qb):
                il, kbi = qb // 16, qb % 16
                rr, nl = kbi // per, kbi % per
                if nl > 0:
                    return qb - 1
                if il > 0:
                    return (il - 1) * 16 + rr * per + per - 1
                return None

            def emit_S(s, n):
                bank = 2 + n % 2
                for j in range(2):
                    qb = 2 * s + j
                    pb = prevblk(qb)
                    kbs = (qb, qb if pb is None else pb)
                    for w2 in range(2):
                        kb = kbs[w2]
                        c0 = j * 256 + w2 * 128
                        P.op("pe", lambda e, bank=bank, c0=c0, kb=kb, qb=qb: e.matmul(
                            ps[:, bank, c0:c0 + 128], KT[:, kb * 128:(kb + 1) * 128], QT[:, qb * 128:(qb + 1) * 128],
                            start=True, stop=True),
                            reads=[("KT", kb // 8), ("QT", qb // 8)], writes=[("ps", bank)])

            def emit_rest(s, n):
                bank = 2 + n % 2
                k4 = n % 4
                P.op("act", lambda e, bank=bank, k4=k4: e.activation(PT[:, k4, :], ps[:, bank, :], AF.Exp, scale=scale),
                     reads=[("ps", bank)], writes=[("PT", k4), ("PTm", k4)])
                P.op("dve", lambda e, k4=k4: e.tensor_tensor(PT[:, k4, :], PT[:, k4, :], mask, ALU.mult),
                     reads=[("PT", k4), "mask"], writes=[("PTm", k4)])
                for j in range(2):
                    qb = 2 * s + j
                    pb = prevblk(qb)
                    obank = 4 + (2 * n + j) % 4
                    o8 = (2 * n + j) % 8
                    il, kbi = qb // 16, qb % 16
                    rr, nl = kbi // per, kbi % per
                    P.op("pe", lambda e, obank=obank, k4=k4, j=j, qb=qb, pb=pb: e.matmul(
                        ps[:, obank, 0:130], PT[:, k4, j * 256:j * 256 + 128], Vaug[:, qb, :], start=True, stop=(pb is None)),
                        reads=[("PTm", k4), ("V", qb // 32), "Vones"], writes=[("ps", obank)])
                    if pb is not None:
                        P.op("pe", lambda e, obank=obank, k4=k4, j=j, pb=pb: e.matmul(
                            ps[:, obank, 0:130], PT[:, k4, j * 256 + 128:j * 256 + 256], Vaug[:, pb, :], start=False, stop=True),
                            reads=[("PTm", k4), ("V", pb // 32), "Vones"], writes=[("ps", obank)])
                    P.op("act", lambda e, obank=obank, o8=o8: e.activation(osb[:, o8, :], ps[:, obank, 0:130], AF.Copy),
                         reads=[("ps", obank)], writes=[("osb", o8)])
                    row0 = b * SEQ + il * NT + rr + r * nl * 128
                    P.dma(c.q2(), oscr_d.ap()[g, row0:row0 + r * 127 + 1:r, 0:130], osb[:, o8, :],
                          reads=[("osb", o8)], writes=[("oscr", b, g, qb)])

            emit_S(0, n)
            for s in range(32):
                if s + 1 < 32:
                    emit_S(s + 1, n + 1)
                emit_rest(s, n)
                n += 1
        if g < 2:
            continue
        for Tq in range(16):
            k2 = Tq % 2
            deps = []
            for T in range(4 * Tq, 4 * Tq + 4):
                il, Tl = T // 16, T % 16
                deps.append(("oscr", b, 0, T))
                deps += [("oscr", b, 1, il * 16 + rr * 4 + Tl // 4) for rr in range(4)]
                deps += [("oscr", b, 2, il * 16 + rr) for rr in range(16)]
            deps = sorted(set(deps))
            ocv = oc[:, k2 * 12:(k2 + 1) * 12, :].rearrange("p (j g) c -> p j g c", g=3)
            osv = osum[:, k2 * 4:(k2 + 1) * 4, :]
            for g in range(3):
                P.dma(c.q2(), ocv[:, :, g, :],
                      oscr_d.ap()[g, b * SEQ + Tq * 512:b * SEQ + (Tq + 1) * 512, 0:130].rearrange("(j p) c -> p j c", p=128),
                      reads=deps, writes=[("oc", k2, g)])
            P.op("dve", lambda e, ocv=ocv, osv=osv: e.tensor_tensor(osv, ocv[:, :, 0, :], ocv[:, :, 1, :], ALU.add),
                 reads=[("oc", k2, 0), ("oc", k2, 1)], writes=[("osum", k2), ("osum2", k2)])
            P.op("dve", lambda e, ocv=ocv, osv=osv: e.tensor_tensor(osv, osv, ocv[:, :, 2, :], ALU.add),
                 reads=[("oc", k2, 2), ("osum", k2)], writes=[("osum2", k2)])
            rv = rec[:, k2 * 4:(k2 + 1) * 4].unsqueeze(2)
            P.op("dve", lambda e, rv=rv, osv=osv: e.reciprocal(rv, osv[:, :, 128:129]),
                 reads=[("osum2", k2)], writes=[("rec", k2)])
            obv = ob[:, k2 * 4:(k2 + 1) * 4, :]
            P.op("dve", lambda e, rv=rv, osv=osv, obv=obv: e.tensor_tensor(obv, osv[:, :, 0:128], rv.to_broadcast([128, 4, 128]), ALU.mult),
                 reads=[("osum2", k2), ("rec", k2)], writes=[("ob", k2)])
            P.dma(c.q2(), ret_dst(b, 4 * Tq, 4), obv, reads=[("ob", k2)], writes=[("ret", b, Tq)])


def emit_attn_B(c, tok_src, ret_dst, mask_d, lam_d, subln_d, cst_d, nb=2):
    P, ar, ps = c.P, c.ar, c.ps
    c.stage_begin()
    toks = [[ar.alloc([128, 64, 128], BF16) for _ in range(3)] for _ in range(2)]
    Vaug = ar.alloc([128, 64, 130], BF16)
    QT = ar.alloc([128, SEQ], BF16)
    KT = ar.alloc([128, SEQ], BF16)
    mask = ar.alloc([128, 512], BF16)
    PT = ar.alloc([128, 2 * 2, 512], BF16)
    lpb = ar.alloc([128, 256], F32)
    lpp = ar.alloc([128, 128], F32)
    lsum = ar.alloc([128, 2], F32)
    lam = ar.alloc([128, 4], F32)
    cst = ar.alloc([128, 2], F32)
    G2 = ar.alloc([128, 128], F32)
    rc = ar.alloc([128, 2 * 4], F32)
    o1 = ar.alloc([128, 2, 128], F32)
    av = ar.alloc([128, 2, 128], F32)
    sqj = ar.alloc([128, 2, 128], F32)
    ss2 = ar.alloc([128, 2], F32)
    rs2 = ar.alloc([128, 2], F32)
    rstd2 = ar.alloc([128, 2], F32)
    ob = ar.alloc([128, 2 * 2, 128], BF16)
    scale = 64.0 ** -0.5
    P.dma("sp", mask, mask_d.ap(), writes=["mask"])
    P.dma("sp", lpb, lam_d.ap().rearrange("a b -> (a b)").partition_broadcast(128), writes=["lpb"])
    P.dma("sp", G2, subln_d.ap().partition_broadcast(128), writes=["G2raw"])
    P.dma("sp", cst, cst_d.ap(), writes=["cst"])
    P.op("pool", lambda e: e.memset(Vaug[:, :, 128:130], 1.0), writes=["Vones"])
    P.op("dve", lambda e: e.tensor_tensor(lpp[:, 0:64], lpb[:, 0:64], lpb[:, 64:128], ALU.mult), reads=["lpb"], writes=["lpp0"])
    P.op("dve", lambda e: e.tensor_tensor(lpp[:, 64:128], lpb[:, 128:192], lpb[:, 192:256], ALU.mult), reads=["lpb"], writes=["lpp1"])
    P.op("dve", lambda e: e.tensor_reduce(lsum[:, 0:2], lpp.rearrange("p (a b) -> p a b", b=64), AX.X, ALU.add),
         reads=["lpp0", "lpp1"], writes=["lsum"])
    P.op("act", lambda e: e.activation(lam[:, 0:2], lsum[:, 0:2], AF.Exp), reads=["lsum"], writes=["lexp"])
    P.op("dve", lambda e: e.tensor_tensor(lam[:, 2:3], lam[:, 0:1], lam[:, 1:2], ALU.subtract), reads=["lexp"], writes=["ldiff"])
    P.op("dve", lambda e: e.tensor_scalar(lam[:, 3:4], lam[:, 2:3], cst[:, 0:1], -1.0, ALU.add, ALU.mult),
         reads=["ldiff", "cst"], writes=["nlam"])
    P.op("dve", lambda e: e.tensor_scalar(G2, G2, cst[:, 1:2], None, ALU.mult), reads=["G2raw", "cst"], writes=["G2"])
    gstep = 0

    def loads(u):
        for w3 in range(3):
            _load_tok(c, toks[u % 2][w3], lambda qi, w3=w3: tok_src(u, w3, qi), ("tok", u % 2, w3))

    loads(0)
    for b in range(nb):
        if b + 1 < nb:
            loads(b + 1)
        _copy_v(c, toks[b % 2][2], Vaug, ("tok", b % 2, 2))
        _transpose_tok(c, toks[b % 2][0], QT, ("tok", b % 2, 0), "QT", 0)
        _transpose_tok(c, toks[b % 2][1], KT, ("tok", b % 2, 1), "KT", 0)
        steps = [(qc, kp) for qc in range(32) for kp in range(qc + 1)]

        def emit_S(i):
            qc, kp = steps[i]
            bufi = (gstep + i) % 2
            q0 = qc * 256
            for kbi in range(2):
                kb = 2 * kp + kbi
                for comp in range(2):
                    bank = bufi * 2 + comp
                    lo, hi = comp * 64, comp * 64 + 64
                    P.op("pe", lambda e, bank=bank, kbi=kbi, kb=kb, lo=lo, hi=hi, q0=q0:
                         e.matmul(ps[:, bank, kbi * 256:(kbi + 1) * 256], KT[lo:hi, kb * 128:(kb + 1) * 128],
                                  QT[lo:hi, q0:q0 + 256], start=True, stop=True),
                         reads=[("KT", kb // 8), ("QT", q0 // 1024)], writes=[("ps", bank)])

        def emit_rest(i):
            qc, kp = steps[i]
            bufi = (gstep + i) % 2
            diag = (kp == qc)
            for comp in range(2):
                bank = bufi * 2 + comp
                P.op("act", lambda e, bank=bank, bufi=bufi, comp=comp: e.activation(PT[:, bufi * 2 + comp, :], ps[:, bank, :], AF.Exp, scale=scale),
                     reads=[("ps", bank)], writes=[("PT", bufi, comp), ("PTa", bufi, comp)])
                if diag:
                    P.op("dve", lambda e, bufi=bufi, comp=comp: e.tensor_tensor(
                        PT[:, bufi * 2 + comp, :].rearrange("p (a b) -> p a b", b=128)[:, 0:4:3, :],
                        PT[:, bufi * 2 + comp, :].rearrange("p (a b) -> p a b", b=128)[:, 0:4:3, :],
                        mask[:, 0:128].unsqueeze(1).to_broadcast([128, 2, 128]), ALU.mult),
                        reads=[("PT", bufi, comp), "mask"], writes=[("PTa", bufi, comp)])
            for kbi in range(2):
                kb = 2 * kp + kbi
                for j in range(2):
                    if diag and kbi == 1 and j == 0:
                        continue
                    first = (kp == 0 and kbi == 0)
                    last = (diag and kbi == j)
                    for comp in range(2):
                        obank = 4 + j * 2 + comp
                        P.op("pe", lambda e, obank=obank, bufi=bufi, comp=comp, kbi=kbi, j=j, kb=kb, first=first, last=last:
                             e.matmul(ps[:, obank, 0:130], PT[:, bufi * 2 + comp, kbi * 256 + j * 128:kbi * 256 + (j + 1) * 128],
                                      Vaug[:, kb, :], start=first, stop=last),
                             reads=[("PTa", bufi, comp), ("V", kb // 32), "Vones"], writes=[("ps", obank)])
            if not diag:
                return
            e2 = qc % 2
            for j in range(2):
                b1, b2 = 4 + j * 2, 4 + j * 2 + 1
                P.op("dve", lambda e, j=j, b1=b1: e.reciprocal(rc[:, j * 4:j * 4 + 1], ps[:, b1, 128:129]),
                     reads=[("ps", b1)], writes=[("rc1", j)])
                P.op("dve", lambda e, j=j, b2=b2: e.reciprocal(rc[:, j * 4 + 1:j * 4 + 2], ps[:, b2, 128:129]),
                     reads=[("ps", b2)], writes=[("rc2", j)])
                P.op("dve", lambda e, j=j: e.tensor_tensor(rc[:, j * 4 + 2:j * 4 + 3], rc[:, j * 4 + 1:j * 4 + 2], lam[:, 3:4], ALU.mult),
                     reads=[("rc2", j), "nlam"], writes=[("rc3", j)])
                P.op("dve", lambda e, j=j, b1=b1: e.tensor_scalar(o1[:, j, :], ps[:, b1, 0:128], rc[:, j * 4:j * 4 + 1], None, ALU.mult),
                     reads=[("ps", b1), ("rc1", j)], writes=[("o1", j)])
                P.op("dve", lambda e, j=j, b2=b2: e.scalar_tensor_tensor(av[:, j, :], ps[:, b2, 0:128], rc[:, j * 4 + 2:j * 4 + 3], o1[:, j, :], ALU.mult, ALU.add),
                     reads=[("ps", b2), ("rc3", j), ("o1", j)], writes=[("av", j)])
                P.op("pool", lambda e, j=j: e.tensor_tensor(sqj[:, j, :], av[:, j, :], av[:, j, :], ALU.mult),
                     reads=[("av", j)], writes=[("sqj", j)])
                P.op("dve", lambda e, j=j: e.tensor_reduce(ss2[:, j:j + 1], sqj[:, j, :], AX.X, ALU.add),
                     reads=[("sqj", j)], writes=[("ss2", j)])
                P.op("dve", lambda e, j=j: e.tensor_scalar(rs2[:, j:j + 1], ss2[:, j:j + 1], 1.0 / 128.0, EPS, ALU.mult, ALU.add),
                     reads=[("ss2", j)], writes=[("rs2", j)])
                P.op("act", lambda e, j=j: e.activation(rs2[:, j:j + 1], rs2[:, j:j + 1], AF.Sqrt),
                     reads=[("rs2", j)], writes=[("rs2b", j)])
                P.op("dve", lambda e, j=j: e.reciprocal(rstd2[:, j:j + 1], rs2[:, j:j + 1]),
                     reads=[("rs2b", j)], writes=[("rstd2", j)])
                P.op("dve", lambda e, j=j, e2=e2: e.scalar_tensor_tensor(ob[:, e2 * 2 + j, :], av[:, j, :], rstd2[:, j:j + 1], G2, ALU.mult, ALU.mult),
                     reads=[("av", j), ("rstd2", j), "G2"], writes=[("ob", e2, j)])
            P.dma("sp", ret_dst(b, 2 * qc, 2), ob[:, e2 * 2:e2 * 2 + 2, :],
                  reads=[("ob", e2, 0), ("ob", e2, 1)], writes=[("ret", b, qc)])

        emit_S(0)
        for i in range(len(steps)):
            if i + 1 < len(steps):
                emit_S(i + 1)
            emit_rest(i)
        gstep += len(steps)


def emit_out(c, retrecv_d, zs_d, wo_d):
    P, ar, ps = c.P, c.ar, c.ps
    c.stage_begin()
    X = c.X
    wst = ar.alloc([128, 8, DM], F32)
    wo = ar.alloc([128, 8, DM], BF16)
    oc = ar.alloc([128, 2, DM], BF16)
    zt = ar.alloc([128, 2, DM], BF16)
    y = ar.alloc([128, 2, DM], BF16)
    yT = ar.alloc([128, 2 * 8, 128], BF16)
    wv = wo_d.ap().rearrange("(c p) n -> p c n", p=128)
    for h2 in range(4):
        P.dma(c.q(), wst[:, h2 * 2:h2 * 2 + 2, :], wv[:, h2 * 2:h2 * 2 + 2, :], writes=[("wst", h2)])
    for cc in range(8):
        if cc % 2 == 0:
            P.op("act", lambda e, cc=cc: e.activation(wo[:, cc, :], wst[:, cc, :], AF.Copy), reads=[("wst", cc // 2)], writes=[("wo", cc)])
        else:
            P.op("pool", lambda e, cc=cc: e.tensor_copy(wo[:, cc, :], wst[:, cc, :]), reads=[("wst", cc // 2)], writes=[("wo", cc)])
    for t in range(NTILE):
        k2 = t % 2
        P.dma("sp", oc[:, k2, :], retrecv_d.ap()[t * 128:(t + 1) * 128, :], writes=[("oc", k2)])
        P.dma("act", zt[:, k2, :], zs_d.ap()[t * 128:(t + 1) * 128, :], writes=[("zt", k2)])
        P.op("dve", lambda e, k2=k2: e.tensor_tensor(y[:, k2, :], oc[:, k2, :], zt[:, k2, :], ALU.mult),
             reads=[("oc", k2), ("zt", k2)], writes=[("y", k2)])
        bank = k2
        tp = ps[:, bank, :].bitcast(BF16)
        for cc in range(8):
            P.op("pe", lambda e, tp=tp, cc=cc, k2=k2: e.transpose(tp[:, cc * 128:(cc + 1) * 128], y[:, k2, cc * 128:(cc + 1) * 128], c.ident),
                 reads=[("y", k2), "ident"], writes=[("ps", bank)])
        P.op("act", lambda e, tp=tp, k2=k2: e.activation(yT[:, k2 * 8:(k2 + 1) * 8, :], tp.rearrange("p (c k) -> p c k", k=128), AF.Copy),
             reads=[("ps", bank)], writes=[("yT", k2)])
        for hf in range(2):
            bank2 = 2 + (t * 2 + hf) % 4
            for cc in range(8):
                P.op("pe", lambda e, bank2=bank2, cc=cc, k2=k2, hf=hf: e.matmul(ps[:, bank2, :], yT[:, k2 * 8 + cc, :], wo[:, cc, hf * 512:(hf + 1) * 512],
                                                                              start=(cc == 0), stop=(cc == 7)),
                     reads=[("yT", k2), ("wo", cc)], writes=[("ps", bank2)])
            P.op("dve", lambda e, bank2=bank2, t=t, hf=hf: e.tensor_tensor(X[:, t, hf * 512:(hf + 1) * 512], X[:, t, hf * 512:(hf + 1) * 512], ps[:, bank2, :], ALU.add),
                 reads=[("ps", bank2), ("X", t)], writes=[("X", t)])


def emit_store_x(c, xo_d):
    xv = xo_d.ap().rearrange("(t p) f -> p t f", p=128)
    for k in range(4):
        c.P.dma(c.q(), xv[:, 4 * k:4 * k + 4, :], c.X[:, 4 * k:4 * k + 4, :],
                reads=[("X", t) for t in range(4 * k, 4 * k + 4)], writes=[("xo", k)])


def emit_final_norm(c, gf_d, y_d):
    P, ar = c.P, c.ar
    c.stage_begin()
    X = c.X
    GF = ar.alloc([128, DM], F32)
    sq = ar.alloc([128, 2, DM], BF16)
    ss = ar.alloc([128, NTILE], F32)
    rs = ar.alloc([128, NTILE], F32)
    rstd = ar.alloc([128, NTILE], F32)
    yo = ar.alloc([128, 2, DM], F32)
    P.dma("sp", GF, gf_d.ap().partition_broadcast(128), writes=["GF"])
    for t in range(NTILE):
        P.op("act", lambda e, t=t: e.activation(sq[:, t % 2, :], X[:, t, :], AF.Square, accum_out=ss[:, t:t + 1]),
             reads=[("X", t)], writes=[("ss", t), ("sq", t % 2)])
    emit_rstd(c, ss, rs, rstd, NTILE, DM, [("ss", t) for t in range(NTILE)])
    yv = y_d.ap().rearrange("(t p) f -> p t f", p=128)
    for t in range(NTILE):
        k2 = t % 2
        P.op("dve", lambda e, t=t, k2=k2: e.scalar_tensor_tensor(yo[:, k2, :], X[:, t, :], rstd[:, t:t + 1], GF, ALU.mult, ALU.mult),
             reads=[("X", t), "rstd", "GF"], writes=[("yo", k2)])
        P.dma(c.q(), yv[:, t, :], yo[:, k2, :], reads=[("yo", k2)], writes=[("y", t)])


def build_program(stages):
    nc = bass.Bass("TRN2", target_bir_lowering=False)
    c = Ctx(nc, need_x=any(st in ("load_x", "out", "final", "projA", "projB") for st in stages))
    EI, EO = "ExternalInput", "ExternalOutput"
    ident_d = c.dram("ident", [128, 128], BF16, EI)
    emit_consts(c, ident_d)
    for s in stages:
        if s == "load_x":
            emit_load_x(c, c.dram("x_in", [NT, DM], F32, EI))
        elif s == "store_x":
            emit_store_x(c, c.dram("x_out", [NT, DM], F32, EO))
        elif s in ("projA", "projB"):
            kind = s[-1]
            ncols = A_IN if kind == "A" else B_IN
            half = 16 if kind == "A" else 8
            ngr = 3 if kind == "A" else 1
            nslot = 9 if kind == "A" else 3
            emit_proj(c, kind,
                      c.dram("w_in", [DM, ncols], F32, EI),
                      c.dram("g_in", [128, 8], F32, EI),
                      c.dram("rope", [128, ngr * NTILE * 2, half], F32, EI),
                      c.dram("send", [nslot, NT, DM], BF16, EO),
                      c.dram("zs_out", [NT, DM], BF16, EO))
        elif s == "attnA":
            recv_d = c.dram("recv", [2, 9, 128, 64, 128], BF16, EI)
            ret_d = c.dram("ret", [2, 128, 64, 128], BF16, EO)
            emit_attn_A(c, lambda b, slot, qi, r=recv_d: r.ap()[b, slot, :, qi * 16:(qi + 1) * 16, :],
                        lambda b, T0, n, r=ret_d: r.ap()[b, :, T0:T0 + n, :],
                        c.dram("oscr", [3, 2 * SEQ, 132], F32, "Internal"),
                        c.dram("mask", [128, 512], BF16, EI))
        elif s == "attnB":
            recv_d = c.dram("recv", [2, 3, 128, 64, 128], BF16, EI)
            ret_d = c.dram("ret", [2, 128, 64, 128], BF16, EO)
            emit_attn_B(c, lambda b, slot, qi, r=recv_d: r.ap()[b, slot, :, qi * 16:(qi + 1) * 16, :],
                        lambda b, T0, n, r=ret_d: r.ap()[b, :, T0:T0 + n, :],
                        c.dram("mask", [128, 512], BF16, EI),
                        c.dram("lam", [4, 64], F32, EI),
                        c.dram("subln", [128], F32, EI),
                        c.dram("cst", [128, 2], F32, EI))
        elif s == "out":
            emit_out(c, c.dram("retrecv", [NT, DM], BF16, EI),
                     c.dram("zs_in", [NT, DM], BF16, EI),
                     c.dram("w_out", [DM, DM], F32, EI))
        elif s == "final":
            emit_final_norm(c, c.dram("gf", [DM], F32, EI), c.dram("y", [NT, DM], F32, EO))
        else:
            raise ValueError(s)
    c.P.finish()
    return nc


def _rope_table(kind, core):
    pos0 = (core % 4) * NT
    p = np.arange(128)
    if kind == "A":
        rot = 32
        inv = (1.0 / (ROPE_THETA ** (np.arange(0, rot, 2, dtype=np.float32) / rot))).astype(np.float32)
        out = np.zeros((128, 3 * NTILE * 2, 16), np.float32)
        for g, r in enumerate(DIL):
            per = NTILE // r
            for tt in range(NTILE):
                rr, nl = tt // per, tt % per
                pos = (pos0 + rr + r * (nl * 128 + p)).astype(np.float32)
                ang = pos[:, None] * inv[None, :]
                out[:, (g * NTILE + tt) * 2, :] = np.cos(ang)
                out[:, (g * NTILE + tt) * 2 + 1, :] = np.sin(ang)
        return out
    rot = 16
    inv = (1.0 / (ROPE_THETA ** (np.arange(0, rot, 2, dtype=np.float32) / rot))).astype(np.float32)
    out = np.zeros((128, NTILE * 2, 8), np.float32)
    for tt in range(NTILE):
        pos = (pos0 + tt * 128 + p).astype(np.float32)
        ang = pos[:, None] * inv[None, :]
        out[:, tt * 2, :] = np.cos(ang)
        out[:, tt * 2 + 1, :] = np.sin(ang)
    return out


_BF = ml_dtypes.bfloat16
_PROGS = {}


def _prog(stages):
    key = tuple(stages)
    if key not in _PROGS:
        _PROGS[key] = build_program(stages)
    return _PROGS[key]


def _run(stages, in_maps):
    nc = _prog(stages)
    res = run_bass_kernel_spmd(nc, in_maps, core_ids=list(range(NCORES)))
    return res.results


def _to_heads(send, nslot):
    S = np.stack(send).reshape(2, 4, nslot, 16, 128, 8, 128)
    S = S.transpose(5, 0, 2, 4, 1, 3, 6)
    return [np.ascontiguousarray(S[h]).reshape(2, nslot, 128, 64, 128) for h in range(NCORES)]


def _to_tokens(ret):
    R = np.stack(ret).reshape(8, 2, 128, 4, 16, 128)
    R = R.transpose(1, 3, 4, 2, 0, 5)
    R = np.ascontiguousarray(R).reshape(NCORES, NT, DM)
    return [R[i] for i in range(NCORES)]


def _glayout(g):
    return np.ascontiguousarray(g.reshape(8, 128).T)


def kernel_unfused(x, a_norm, a_w_in, a_w_out, b_norm, b_w_in, b_lambda, b_subln, b_w_out, final_norm):
    import math
    x = np.asarray(x, np.float32)
    xs = [np.ascontiguousarray(s) for s in np.split(x.reshape(-1, DM), NCORES, axis=0)]
    ident = np.eye(128, dtype=np.float32).astype(_BF)
    kk = np.arange(128)[:, None]
    qq = np.arange(128)[None, :]
    mask = np.concatenate([(kk <= qq), (kk >= qq)] * 2, axis=1).astype(np.float32).astype(_BF)
    ropeA = [_rope_table("A", i) for i in range(NCORES)]
    ropeB = [_rope_table("B", i) for i in range(NCORES)]
    w_in = [np.asarray(a_w_in[0]), np.asarray(b_w_in[0]), np.asarray(a_w_in[1]), np.asarray(b_w_in[1])]
    w_out = [np.asarray(a_w_out[0]), np.asarray(b_w_out[0]), np.asarray(a_w_out[1]), np.asarray(b_w_out[1])]
    gs = [_glayout(np.asarray(a_norm[0])), _glayout(np.asarray(b_norm[0])),
          _glayout(np.asarray(a_norm[1])), _glayout(np.asarray(b_norm[1]))]
    kinds = ["A", "B", "A", "B"]

    def proj_inputs(layer, i):
        kd = kinds[layer]
        return {"w_in": np.ascontiguousarray(w_in[layer], dtype=np.float32), "g_in": gs[layer],
                "rope": ropeA[i] if kd == "A" else ropeB[i]}

    res = _run(["load_x", "projA"], [dict(ident=ident, x_in=xs[i], **proj_inputs(0, i)) for i in range(NCORES)])
    send = [r["send"] for r in res]
    zs = [r["zs_out"] for r in res]
    for layer in range(4):
        kd = kinds[layer]
        recv = _to_heads(send, 9 if kd == "A" else 3)
        if kd == "A":
            res = _run(["attnA"], [dict(ident=ident, recv=recv[h], mask=mask) for h in range(NCORES)])
        else:
            j = layer // 2
            lam_init = 0.8 - 0.6 * math.exp(-0.3 * layer)
            cst = np.tile(np.array([[lam_init, 1.0 - lam_init]], np.float32), (128, 1))
            res = _run(["attnB"], [dict(ident=ident, recv=recv[h], mask=mask,
                                       lam=np.ascontiguousarray(b_lambda[j], dtype=np.float32),
                                       subln=np.ascontiguousarray(b_subln[j], dtype=np.float32), cst=cst)
                                  for h in range(NCORES)])
        retrecv = _to_tokens([r["ret"] for r in res])
        wo = np.ascontiguousarray(w_out[layer], dtype=np.float32)
        if layer < 3:
            nk = kinds[layer + 1]
            res = _run(["load_x", "out", "proj" + nk, "store_x"],
                       [dict(ident=ident, x_in=xs[i], retrecv=retrecv[i], zs_in=zs[i], w_out=wo,
                             **proj_inputs(layer + 1, i)) for i in range(NCORES)])
            xs = [r["x_out"] for r in res]
            send = [r["send"] for r in res]
            zs = [r["zs_out"] for r in res]
        else:
            res = _run(["load_x", "out", "final"],
                       [dict(ident=ident, x_in=xs[i], retrecv=retrecv[i], zs_in=zs[i], w_out=wo,
                             gf=np.ascontiguousarray(final_norm, dtype=np.float32)) for i in range(NCORES)])
            ys = [r["y"] for r in res]
    return np.concatenate(ys, axis=0).reshape(2, SEQ, DM).astype(np.float32)


class _V:
    def __init__(self, ap):
        self._ap = ap

    def ap(self):
        return self._ap


def _x_scope_begin(c):
    c.P.barrier()
    c.base = c.base0
    c.ar.reset(c.base0)
    c.X = c.ar.alloc([128, NTILE, DM], F32)
    c.base = c.ar.mark()


def _x_scope_end(c):
    c.base = c.base0
    c.X = None


def build_fused():
    nc = bass.Bass("TRN2", target_bir_lowering=False)
    c = Ctx(nc, need_x=False)
    c.base0 = c.base
    EI, EO, IN = "ExternalInput", "ExternalOutput", "Internal"
    kinds = ["A", "B", "A", "B"]
    ident_d = c.dram("ident", [128, 128], BF16, EI)
    x_in = c.dram("x_in", [4, NT, DM], F32, EI)
    y_d = c.dram("y", [4, NT, DM], F32, EO)
    w_in = [c.dram("w_in%d" % l, [DM, A_IN if kinds[l] == "A" else B_IN], F32, EI) for l in range(4)]
    g_in = [c.dram("g_in%d" % l, [128, 8], F32, EI) for l in range(4)]
    w_out = [c.dram("w_out%d" % l, [DM, DM], F32, EI) for l in range(4)]
    ropeA = c.dram("ropeA", [4, 128, 3 * NTILE * 2, 16], F32, EI)
    ropeB = c.dram("ropeB", [4, 128, NTILE * 2, 8], F32, EI)
    mask_d = c.dram("mask", [128, 512], BF16, EI)
    lam_d = [c.dram("lam%d" % j, [4, 64], F32, EI) for j in range(2)]
    subln_d = [c.dram("subln%d" % j, [128], F32, EI) for j in range(2)]
    cst_d = [c.dram("cst%d" % j, [128, 2], F32, EI) for j in range(2)]
    gf_d = c.dram("gf", [DM], F32, EI)
    xs = c.dram("xs", [4, NT, DM], F32, IN)
    send = c.dram("send", [4, 9, NT, DM], BF16, IN)
    zs = c.dram("zs", [4, NT, DM], BF16, IN)
    retr = c.dram("retr", [4, NT, DM], BF16, IN)
    oscr = c.dram("oscr", [3, SEQ, 132], F32, IN)
    emit_consts(c, ident_d)

    def proj(layer, il):
        kd = kinds[layer]
        rope = ropeA if kd == "A" else ropeB
        emit_proj(c, kd, w_in[layer], g_in[layer], _V(rope.ap()[il]), _V(send.ap()[il]), _V(zs.ap()[il]))

    for il in range(4):
        _x_scope_begin(c)
        emit_load_x(c, _V(x_in.ap()[il]))
        proj(0, il)
    for layer in range(4):
        _x_scope_end(c)
        for h in range(8):
            def tok_src(b, slot, qi, h=h):
                return send.ap()[qi, slot, :, h * 128:(h + 1) * 128].rearrange("(k p) d -> p k d", p=128)

            def ret_dst(b, T0, n, h=h):
                il, Tl = T0 // 16, T0 % 16
                return retr.ap()[il, Tl * 128:(Tl + n) * 128, h * 128:(h + 1) * 128].rearrange("(j p) d -> p j d", p=128)

            if kinds[layer] == "A":
                emit_attn_A(c, tok_src, ret_dst, oscr, mask_d, nb=1)
            else:
                j = layer // 2
                emit_attn_B(c, tok_src, ret_dst, mask_d, lam_d[j], subln_d[j], cst_d[j], nb=1)
        for il in range(4):
            _x_scope_begin(c)
            emit_load_x(c, _V(x_in.ap()[il] if layer == 0 else xs.ap()[il]))
            emit_out(c, _V(retr.ap()[il]), _V(zs.ap()[il]), w_out[layer])
            if layer < 3:
                proj(layer + 1, il)
                emit_store_x(c, _V(xs.ap()[il]))
            else:
                emit_final_norm(c, gf_d, _V(y_d.ap()[il]))
    c.P.finish()
    return nc


def kernel_fused(x, a_norm, a_w_in, a_w_out, b_norm, b_w_in, b_lambda, b_subln, b_w_out, final_norm):
    import math
    x = np.asarray(x, np.float32)
    ident = np.eye(128, dtype=np.float32).astype(_BF)
    kk = np.arange(128)[:, None]
    qq = np.arange(128)[None, :]
    mask = np.concatenate([(kk <= qq), (kk >= qq)] * 2, axis=1).astype(np.float32).astype(_BF)
    ropeA = np.stack([_rope_table("A", i) for i in range(4)])
    ropeB = np.stack([_rope_table("B", i) for i in range(4)])
    f32 = lambda a: np.ascontiguousarray(np.asarray(a), dtype=np.float32)
    common = dict(ident=ident, mask=mask, ropeA=ropeA, ropeB=ropeB, gf=f32(final_norm),
                  w_in0=f32(a_w_in[0]), w_in1=f32(b_w_in[0]), w_in2=f32(a_w_in[1]), w_in3=f32(b_w_in[1]),
                  w_out0=f32(a_w_out[0]), w_out1=f32(b_w_out[0]), w_out2=f32(a_w_out[1]), w_out3=f32(b_w_out[1]),
                  g_in0=_glayout(np.asarray(a_norm[0])), g_in1=_glayout(np.asarray(b_norm[0])),
                  g_in2=_glayout(np.asarray(a_norm[1])), g_in3=_glayout(np.asarray(b_norm[1])),
                  lam0=f32(b_lambda[0]), lam1=f32(b_lambda[1]), subln0=f32(b_subln[0]), subln1=f32(b_subln[1]))
    for j, layer in enumerate((1, 3)):
        lam_init = 0.8 - 0.6 * math.exp(-0.3 * layer)
        common["cst%d" % j] = np.tile(np.array([[lam_init, 1.0 - lam_init]], np.float32), (128, 1))
    in_maps = [dict(common, x_in=np.ascontiguousarray(x[c % 2].reshape(4, NT, DM))) for c in range(NCORES)]
    if "fused" not in _PROGS:
        _PROGS["fused"] = build_fused()
    res = run_bass_kernel_spmd(_PROGS["fused"], in_maps, core_ids=list(range(NCORES))).results
    return np.stack([res[0]["y"].reshape(SEQ, DM), res[1]["y"].reshape(SEQ, DM)]).astype(np.float32)


MODE = "fused"


def kernel(**inputs):
    if MODE == "fused":
        return kernel_fused(**inputs)
    return kernel_unfused(**inputs)
```

```python
import contextlib
import numpy as np
import ml_dtypes
import concourse.bass as bass
import concourse.mybir as mybir
from concourse.bass_utils import run_bass_kernel_spmd

F32 = mybir.dt.float32
BF16 = mybir.dt.bfloat16
AF = mybir.ActivationFunctionType
ALU = mybir.AluOpType
AX = mybir.AxisListType


class _Op:
    __slots__ = ("eng", "fn", "reads", "writes", "dma", "signal", "sigval", "sem", "waits", "barrier")

    def __init__(self, eng, fn, reads, writes, dma=False, barrier=False):
        self.eng = eng
        self.fn = fn
        self.reads = reads
        self.writes = writes
        self.dma = dma
        self.signal = False
        self.sigval = 0
        self.sem = None
        self.waits = []
        self.barrier = barrier


class Prog:
    ENGS = ("pe", "act", "dve", "pool", "sp")

    def __init__(self, nc, n_dma_sems=6):
        self.nc = nc
        self.ops = []
        self.stack = contextlib.ExitStack()
        self.n_dma_sems = n_dma_sems
        self._uid = 0

    def sb(self, name, shape, dtype):
        return self.stack.enter_context(self.nc.sbuf_tensor(name, list(shape), dtype))

    def ps(self, name, shape, dtype):
        return self.stack.enter_context(self.nc.psum_tensor(name, list(shape), dtype))

    def op(self, eng, fn, reads=(), writes=()):
        self.ops.append(_Op(eng, fn, tuple(reads), tuple(writes)))

    def dma(self, q, out, in_, reads=(), writes=(), **kw):
        self.ops.append(_Op(q, lambda e: e.dma_start(out=out, in_=in_, **kw), tuple(reads), tuple(writes), dma=True))

    def barrier(self):
        self.ops.append(_Op(None, None, (), (), barrier=True))

    def finish(self):
        nc = self.nc
        self.barrier()
        ops = self.ops
        last_writer = {}
        readers = {}
        deps = [None] * len(ops)
        for j, o in enumerate(ops):
            if o.barrier:
                continue
            d = set()
            xr = tuple(t for t in o.reads if isinstance(t, tuple) and t[0] == "ps")
            if xr:
                o.reads = tuple(t for t in o.reads if t not in xr)
                o.writes = tuple(o.writes) + tuple(t for t in xr if t not in o.writes)
            for t in o.reads:
                w = last_writer.get(t)
                if w is not None:
                    d.add(w)
            for t in o.writes:
                w = last_writer.get(t)
                if w is not None:
                    d.add(w)
                for r in readers.get(t, ()):
                    d.add(r)
            for t in o.writes:
                last_writer[t] = j
                readers[t] = []
            for t in o.reads:
                readers.setdefault(t, []).append(j)
            d.discard(j)
            keep = []
            for i in d:
                p = ops[i]
                if not p.dma and not o.dma and p.eng == o.eng:
                    if o.eng in ("pe", "sp"):
                        continue
                keep.append(i)
            best = {}
            kk = []
            for i in keep:
                p = ops[i]
                if p.dma:
                    kk.append(i)
                elif i > best.get(p.eng, -1):
                    best[p.eng] = i
            keep = kk + list(best.values())
            deps[j] = keep
            for i in keep:
                ops[i].signal = True
        last_on = {}
        for j, o in enumerate(ops):
            if o.barrier:
                for e, i in last_on.items():
                    ops[i].signal = True
            elif not o.dma:
                last_on[o.eng] = j
        sems = {}
        for e in self.ENGS:
            sems[e] = self.stack.enter_context(nc.semaphore("s_" + e))
        dma_sems = {}
        for q in ("sp", "act", "pool"):
            dma_sems[q] = [self.stack.enter_context(nc.semaphore("d_%s%d" % (q, k))) for k in range(self.n_dma_sems)]
        cnt = {e: 0 for e in self.ENGS}
        dcnt = {q: 0 for q in dma_sems}
        dtot = {}
        prev_dma_wait = {}
        for j, o in enumerate(ops):
            if o.barrier:
                continue
            if o.dma:
                k = dcnt[o.eng] % self.n_dma_sems
                dcnt[o.eng] += 1
                s = dma_sems[o.eng][k]
                prev = dtot.get((o.eng, k), 0)
                if prev:
                    prev_dma_wait[j] = (s, prev)
                dtot[(o.eng, k)] = prev + 16
                o.sem = s
                o.sigval = prev + 16
            elif o.signal:
                cnt[o.eng] += 1
                o.sem = sems[o.eng]
                o.sigval = cnt[o.eng]
        known = {e: {} for e in self.ENGS}
        cur_sig = {e: 0 for e in self.ENGS}
        cur_dma = {}
        per_eng = {e: [] for e in self.ENGS}
        for j, o in enumerate(ops):
            if o.barrier:
                for e in self.ENGS:
                    w = []
                    for e2 in self.ENGS:
                        if e2 != e and cur_sig[e2] > known[e].get(id(sems[e2]), 0):
                            w.append((sems[e2], cur_sig[e2]))
                            known[e][id(sems[e2])] = cur_sig[e2]
                    for (s, v) in cur_dma.values():
                        if v > known[e].get(id(s), 0):
                            w.append((s, v))
                            known[e][id(s)] = v
                    if w:
                        per_eng[e].append((w, None))
                continue
            need = {}
            for i in deps[j]:
                p = ops[i]
                key = id(p.sem)
                if p.sigval > need.get(key, (None, 0))[1]:
                    need[key] = (p.sem, p.sigval)
            if j in prev_dma_wait:
                s, v = prev_dma_wait[j]
                if v > need.get(id(s), (None, 0))[1]:
                    need[id(s)] = (s, v)
            w = []
            for key, (s, v) in need.items():
                if v > known[o.eng].get(key, 0):
                    w.append((s, v))
                    known[o.eng][key] = v
            per_eng[o.eng].append((w, o))
            if o.dma:
                cur_dma[id(o.sem)] = (o.sem, o.sigval)
            elif o.signal:
                cur_sig[o.eng] = o.sigval
        engobj = {"pe": "tensor", "act": "scalar", "dve": "vector", "pool": "gpsimd", "sp": "sync"}
        block = self.stack.enter_context(nc.Block())

        def make(e):
            def body(eng):
                for w, o in per_eng[e]:
                    for (s, v) in w:
                        eng.wait_ge(s, v)
                    if o is None:
                        continue
                    ins = o.fn(eng)
                    if o.dma:
                        ins.then_inc(o.sem, 16)
                    elif o.signal:
                        ins.then_inc(o.sem, 1)
            return body

        for e in self.ENGS:
            getattr(block, engobj[e])(make(e))
        self.n_ops = len(ops)
        self.stack.close()


NCORES = 8
NT = 2048
NTILE = 16
DM = 1024
SEQ = 8192
EPS = 1e-6
ROPE_THETA = 500000.0
DIL = (1, 4, 16)
A_IN = 10240
B_IN = 4096


DBG = {}


class Arena:
    def __init__(self, P, nbytes):
        self.t = P.sb("arena", [128, nbytes // 2], BF16)
        self.n = nbytes // 2
        self.off = 0

    def mark(self):
        return self.off

    def reset(self, m):
        self.off = m

    def alloc(self, shape, dtype):
        esz = 4 if dtype == F32 else 2
        n = 1
        for s in shape[1:]:
            n *= s
        nb16 = n * esz // 2
        self.off = (self.off + 15) // 16 * 16
        assert self.off + nb16 <= self.n, "arena overflow %d + %d > %d" % (self.off, nb16, self.n)
        ap = self.t[0:shape[0], self.off:self.off + nb16]
        self.off += nb16
        if dtype == F32:
            ap = ap.bitcast(F32)
        if len(shape) == 3:
            ap = ap.rearrange("p (a b) -> p a b", b=shape[2])
        elif len(shape) == 4:
            ap = ap.rearrange("p (a b c) -> p a b c", b=shape[2], c=shape[3])
        return ap


class Ctx:
    def __init__(self, nc, need_x=True):
        self.nc = nc
        self.P = Prog(nc)
        self.ar = Arena(self.P, 204 * 1024)
        self.ps = self.P.ps("ps", [128, 8, 512], F32)
        self.X = self.ar.alloc([128, NTILE, DM], F32) if need_x else None
        self.ident = self.ar.alloc([128, 128], BF16)
        self.base = self.ar.mark()
        self.uid = 0
        self.dq = 0

    def q(self):
        return "sp"

    def q2(self):
        return "sp"

    def stage_begin(self):
        self.P.barrier()
        self.ar.reset(self.base)

    def dram(self, name, shape, dtype, kind):
        return self.nc.dram_tensor(name, list(shape), dtype, kind=kind)


def emit_consts(c, ident_d):
    c.P.dma("sp", c.ident, ident_d.ap(), writes=["ident"])


def emit_load_x(c, x_d):
    xv = x_d.ap().rearrange("(t p) f -> p t f", p=128)
    for k in range(4):
        c.P.dma(c.q(), c.X[:, 4 * k:4 * k + 4, :], xv[:, 4 * k:4 * k + 4, :],
                writes=[("X", t) for t in range(4 * k, 4 * k + 4)])


def emit_rstd(c, ss, rs, rstd, n, denom, reads):
    P = c.P
    P.op("dve", lambda e: e.tensor_scalar(rs, ss, 1.0 / denom, EPS, ALU.mult, ALU.add), reads=reads, writes=["rs"])
    P.op("act", lambda e: e.activation(rs, rs, AF.Sqrt), reads=["rs"], writes=["rs2"])
    P.op("dve", lambda e: e.reciprocal(rstd, rs), reads=["rs2"], writes=["rstd"])


def emit_proj(c, kind, w_d, g_d, rope_d, send_d, zs_d):
    P, ar, ps = c.P, c.ar, c.ps
    c.stage_begin()
    ncols = A_IN if kind == "A" else B_IN
    nblk = ncols // 512
    if kind == "A":
        nh, hd, half = 4, 128, 16
    else:
        nh, hd, half = 8, 64, 8
    hnT = ar.alloc([128, 8, NT], BF16)
    sq = ar.alloc([128, 2, DM], BF16)
    hn = ar.alloc([128, 2, DM], BF16)
    ss = ar.alloc([128, NTILE], F32)
    rs = ar.alloc([128, NTILE], F32)
    rstd = ar.alloc([128, NTILE], F32)
    G = ar.alloc([128, 8], F32)
    ngr = 3 if kind == "A" else 1
    rope = ar.alloc([128, ngr * NTILE * 2, half], F32)
    wst = ar.alloc([128, 2 * 8, 512], F32)
    wb = ar.alloc([128, 2 * 8, 512], BF16)
    st = ar.alloc([128, 4, 512], BF16)
    tmp = ar.alloc([128, 4 * 4, nh * half], F32)
    X, ident = c.X, c.ident

    P.dma("sp", G, g_d.ap(), writes=["G"])
    P.dma("sp", rope, rope_d.ap(), writes=["rope"])
    for t in range(NTILE):
        P.op("act", lambda e, t=t: e.activation(sq[:, t % 2, :], X[:, t, :], AF.Square, accum_out=ss[:, t:t + 1]),
             reads=[("X", t)], writes=[("ss", t), ("sq", t % 2)])
    lvl = DBG.get("lvl", 9)
    if lvl >= 2:
        emit_rstd(c, ss, rs, rstd, NTILE, DM, [("ss", t) for t in range(NTILE)])
    for t in range(NTILE if lvl >= 3 else 0):
        P.op("act", lambda e, t=t: e.activation(hn[:, t % 2, :], X[:, t, :], AF.Copy, scale=rstd[:, t:t + 1]),
             reads=[("X", t), "rstd"], writes=[("hn", t % 2)])
        if lvl < 4:
            continue
        bank = t % 2
        tp = ps[:, bank, :].bitcast(BF16)
        for cc in range(8):
            P.op("pe", lambda e, t=t, cc=cc, tp=tp: e.transpose(tp[:, cc * 128:(cc + 1) * 128],
                                                             hn[:, t % 2, cc * 128:(cc + 1) * 128], ident),
                 reads=[("hn", t % 2), "ident"], writes=[("ps", bank)])
        if lvl < 5:
            continue
        P.op("dve", lambda e, t=t, tp=tp: e.tensor_copy(hnT[:, :, t * 128:(t + 1) * 128],
                                                        tp.rearrange("p (c k) -> p c k", k=128)),
             reads=[("ps", bank)], writes=["hnT"])

    wv = w_d.ap().rearrange("(c p) n -> p c n", p=128)
    cnt = 0
    def wload(blk):
        slot = blk % 2
        for h2 in range(2):
            P.dma("sp", wst[:, slot * 8 + h2 * 4:slot * 8 + h2 * 4 + 4, :],
                  wv[:, h2 * 4:h2 * 4 + 4, blk * 512:(blk + 1) * 512], writes=[("wst", slot, h2)])

    blks = list(DBG.get("blks", range(nblk)))
    wload(blks[0])
    for bi, blk in enumerate(blks):
        slot = blk % 2
        if bi + 1 < len(blks):
            wload(blks[bi + 1])
        for cc in range(8):
            eng = "act" if cc % 2 == 0 else "pool"
            if eng == "act":
                P.op("act", lambda e, cc=cc, slot=slot: e.activation(wb[:, slot * 8 + cc, :], wst[:, slot * 8 + cc, :],
                                                                   AF.Copy, scale=G[:, cc:cc + 1]),
                     reads=[("wst", slot, cc // 4), "G"], writes=[("wb", slot, cc)])
            else:
                P.op("pool", lambda e, cc=cc, slot=slot: e.tensor_scalar(wb[:, slot * 8 + cc, :], wst[:, slot * 8 + cc, :],
                                                                       G[:, cc:cc + 1], None, ALU.mult),
                     reads=[("wst", slot, cc // 4), "G"], writes=[("wb", slot, cc)])
        if kind == "A":
            sec = blk // 2
            hh = blk % 2
            if sec < 9:
                g, which = sec // 3, sec % 3
            else:
                g, which = 0, 3
        else:
            which = blk // 2
            hh = blk % 2
            g = 0
        r = DIL[g] if kind == "A" else 1
        for tt in range(NTILE):
            bank = 2 + (cnt % 6)
            k4 = cnt % 4
            cnt += 1
            per = NTILE // r
            rr, nl = tt // per, tt % per
            start = rr + r * 128 * nl
            stop = start + r * 127 + 1
            for cc in range(8):
                P.op("pe", lambda e, cc=cc, slot=slot, bank=bank, start=start, stop=stop, r=r:
                     e.matmul(ps[:, bank, :], hnT[:, cc, start:stop:r], wb[:, slot * 8 + cc, :],
                              start=(cc == 0), stop=(cc == 7)),
                     reads=["hnT", ("wb", slot, cc)], writes=[("ps", bank)])
            pv = ps[:, bank, :].rearrange("p (h d) -> p h d", d=hd)
            sv = st[:, k4, :].rearrange("p (h d) -> p h d", d=hd)
            if which in (0, 1):
                ri = (g * NTILE + tt) * 2
                if DBG.get("nobc"):
                    C = rope[:, 0:nh, :]
                    S = rope[:, 0:nh, :]
                else:
                    C = rope[:, ri, :].unsqueeze(1).to_broadcast([128, nh, half])
                    S = rope[:, ri + 1, :].unsqueeze(1).to_broadcast([128, nh, half])
                tv = [tmp[:, k4 * 4 + i, :].rearrange("p (h d) -> p h d", d=half) for i in range(4)]
                x1, x2 = pv[:, :, 0:half], pv[:, :, half:2 * half]
                rl = DBG.get("rope_lvl", 9)
                P.op("act", lambda e, pv=pv, sv=sv: e.activation(sv[:, :, 2 * half:], pv[:, :, 2 * half:], AF.Copy),
                     reads=[("ps", bank)], writes=[("st", k4, "p")])
                ser = [("st", k4, "p")] if DBG.get("ser") else []
                if rl >= 1:
                  P.op("dve", lambda e, tv=tv, x1=x1, C=C: e.tensor_tensor(tv[0], x1, C, ALU.mult),
                     reads=[("ps", bank), "rope"] + ser, writes=[("tmp", k4, 0)])
                if rl >= 2:
                  P.op("dve", lambda e, tv=tv, x2=x2, S=S: e.tensor_tensor(tv[1], x2, S, ALU.mult),
                     reads=[("ps", bank), "rope"], writes=[("tmp", k4, 1)])
                if rl >= 3:
                  P.op("dve", lambda e, tv=tv, x2=x2, C=C: e.tensor_tensor(tv[2], x2, C, ALU.mult),
                     reads=[("ps", bank), "rope"], writes=[("tmp", k4, 2)])
                if rl >= 3:
                  P.op("dve", lambda e, tv=tv, x1=x1, S=S: e.tensor_tensor(tv[3], x1, S, ALU.mult),
                     reads=[("ps", bank), "rope"], writes=[("tmp", k4, 3)])
                if rl >= 4:
                  P.op(DBG.get("rope_eng", "dve"), lambda e, tv=tv, sv=sv: e.tensor_tensor(sv[:, :, 0:half], tv[0], tv[1], ALU.subtract),
                     reads=[("tmp", k4, 0), ("tmp", k4, 1)], writes=[("st", k4, "r1")])
                if rl >= 5:
                  P.op(DBG.get("rope_eng", "dve"), lambda e, tv=tv, sv=sv: e.tensor_tensor(sv[:, :, half:2 * half], tv[2], tv[3], ALU.add),
                     reads=[("tmp", k4, 2), ("tmp", k4, 3)], writes=[("st", k4, "r2")])
                streads = [("st", k4, "r1"), ("st", k4, "r2"), ("st", k4, "p")]
            elif which == 2:
                streads = [("st", k4, "r1"), ("st", k4, "r2"), ("st", k4, "p")]
                P.op("act", lambda e, k4=k4, bank=bank: e.activation(st[:, k4, :], ps[:, bank, :], AF.Copy),
                     reads=[("ps", bank)], writes=streads)
            else:
                streads = [("st", k4, "r1"), ("st", k4, "r2"), ("st", k4, "p")]
                P.op("act", lambda e, k4=k4, bank=bank: e.activation(st[:, k4, :], ps[:, bank, :], AF.Silu),
                     reads=[("ps", bank)], writes=streads)
            if which < 3:
                slotidx = g * 3 + which if kind == "A" else which
                dst = send_d.ap()[slotidx, tt * 128:(tt + 1) * 128, hh * 512:(hh + 1) * 512]
                P.dma(c.q2(), dst, st[:, k4, :], reads=streads, writes=[("send", slotidx, hh, tt)])
            else:
                P.dma(c.q2(), zs_d.ap()[tt * 128:(tt + 1) * 128, hh * 512:(hh + 1) * 512], st[:, k4, :],
                      reads=streads, writes=[("zs", hh, tt)])


def _load_tok(c, dst, src_of_quarter, name):
    for qi in range(4):
        c.P.dma(c.q(), dst[:, qi * 16:(qi + 1) * 16, :], src_of_quarter(qi), writes=[(name, qi)])


def _transpose_tok(c, src, dstT, name, nameT, bank0, dstT2=None):
    P, ps = c.P, c.ps
    for grp in range(8):
        bank = bank0 + grp % 2
        tp = ps[:, bank, :].bitcast(BF16)
        for k in range(8):
            blk = grp * 8 + k
            P.op("pe", lambda e, tp=tp, k=k, blk=blk: e.transpose(tp[:, k * 128:(k + 1) * 128], src[:, blk, 0:128], c.ident),
                 reads=[(name, blk // 16), "ident"], writes=[("ps", bank)])
        if dstT2 is not None:
            P.op("dve", lambda e, tp=tp, grp=grp: e.tensor_copy(dstT[0:64, grp * 1024:(grp + 1) * 1024], tp[0:64, :]),
                 reads=[("ps", bank), "qzero"], writes=[(nameT, grp)])
            P.op("act", lambda e, tp=tp, grp=grp: e.activation(dstT2[64:128, grp * 1024:(grp + 1) * 1024], tp[64:128, :], AF.Copy),
                 reads=[("ps", bank), "qzero"], writes=[(nameT + "2", grp)])
        elif grp % 2 == 0:
            P.op("dve", lambda e, tp=tp, grp=grp: e.tensor_copy(dstT[:, grp * 1024:(grp + 1) * 1024], tp),
                 reads=[("ps", bank)], writes=[(nameT, grp)])
        else:
            P.op("act", lambda e, tp=tp, grp=grp: e.activation(dstT[:, grp * 1024:(grp + 1) * 1024], tp, AF.Copy),
                 reads=[("ps", bank)], writes=[(nameT, grp)])


def _copy_v(c, Vtok, Vaug, vname):
    for hf in range(2):
        c.P.op("pool", lambda e, hf=hf: e.tensor_copy(Vaug[:, hf * 32:(hf + 1) * 32, 0:128], Vtok[:, hf * 32:(hf + 1) * 32, :]),
               reads=[(vname, 2 * hf), (vname, 2 * hf + 1)], writes=[("V", hf)])


def emit_attn_A(c, tok_src, ret_dst, oscr_d, mask_d, nb=2):
    P, ar, ps = c.P, c.ar, c.ps
    c.stage_begin()
    toks = [[ar.alloc([128, 64, 128], BF16) for _ in range(3)] for _ in range(2)]
    Vaug = ar.alloc([128, 64, 130], BF16)
    QT = ar.alloc([128, SEQ], BF16)
    KT = ar.alloc([128, SEQ], BF16)
    mask = ar.alloc([128, 512], BF16)
    PT = ar.alloc([128, 4, 512], BF16)
    osb = ar.alloc([128, 8, 130], F32)
    oc = ar.alloc([128, 2 * 4 * 3, 130], F32)
    osum = ar.alloc([128, 2 * 4, 130], F32)
    rec = ar.alloc([128, 2 * 4], F32)
    ob = ar.alloc([128, 2 * 4, 128], BF16)
    scale = 128.0 ** -0.5
    P.dma("sp", mask, mask_d.ap(), writes=["mask"])
    P.op("pool", lambda e: e.memset(Vaug[:, :, 128:130], 1.0), writes=["Vones"])
    n = 0
    units = [(b, g) for b in range(nb) for g in range(3)]

    def loads(u):
        b_, g_ = units[u]
        for w3 in range(3):
            _load_tok(c, toks[u % 2][w3], lambda qi, w3=w3: tok_src(b_, g_ * 3 + w3, qi), ("tok", u % 2, w3))

    loads(0)
    for u, (b, g) in enumerate(units):
        if True:
            r = DIL[g]
            per = NTILE // r
            if u + 1 < len(units):
                loads(u + 1)
            _copy_v(c, toks[u % 2][2], Vaug, ("tok", u % 2, 2))
            _transpose_tok(c, toks[u % 2][0], QT, ("tok", u % 2, 0), "QT", 0)
            _transpose_tok(c, toks[u % 2][1], KT, ("tok", u % 2, 1), "KT", 0)

            def prevblk(qb):
                il, kbi = qb // 16, qb % 16
                rr, nl = kbi // per, kbi % per
                if nl > 0:
                    return qb - 1
                if il > 0:
                    return (il - 1) * 16 + rr * per + per - 1
                return None

            def emit_S(s, n):
                bank = 2 + n % 3
                for j in range(2):
                    qb = 2 * s + j
                    pb = prevblk(qb)
                    kbs = (qb, qb if pb is None else pb)
                    for w2 in range(2):
                        kb = kbs[w2]
                        c0 = j * 256 + w2 * 128
                        P.op("pe", lambda e, bank=bank, c0=c0, kb=kb, qb=qb: e.matmul(
                            ps[:, bank, c0:c0 + 128], KT[:, kb * 128:(kb + 1) * 128], QT[:, qb * 128:(qb + 1) * 128],
                            start=True, stop=True),
                            reads=[("KT", kb // 8), ("QT", qb // 8)], writes=[("ps", bank)])

            def emit_rest(s, n):
                bank = 2 + n % 3
                k4 = n % 4
                P.op("act", lambda e, bank=bank, k4=k4: e.activation(PT[:, k4, :], ps[:, bank, :], AF.Exp, scale=scale),
                     reads=[("ps", bank)], writes=[("PT", k4), ("PTm", k4)])
                P.op("dve", lambda e, k4=k4: e.tensor_tensor(PT[:, k4, :], PT[:, k4, :], mask, ALU.mult),
                     reads=[("PT", k4), "mask"], writes=[("PTm", k4)])
                for j in range(2):
                    qb = 2 * s + j
                    pb = prevblk(qb)
                    obank = 5 + (2 * n + j) % 3
                    o8 = (2 * n + j) % 8
                    il, kbi = qb // 16, qb % 16
                    rr, nl = kbi // per, kbi % per
                    P.op("pe", lambda e, obank=obank, k4=k4, j=j, qb=qb, pb=pb: e.matmul(
                        ps[:, obank, 0:130], PT[:, k4, j * 256:j * 256 + 128], Vaug[:, qb, :], start=True, stop=(pb is None)),
                        reads=[("PTm", k4), ("V", qb // 32), "Vones"], writes=[("ps", obank)])
                    if pb is not None:
                        P.op("pe", lambda e, obank=obank, k4=k4, j=j, pb=pb: e.matmul(
                            ps[:, obank, 0:130], PT[:, k4, j * 256 + 128:j * 256 + 256], Vaug[:, pb, :], start=False, stop=True),
                            reads=[("PTm", k4), ("V", pb // 32), "Vones"], writes=[("ps", obank)])
                    P.op("dve", lambda e, obank=obank, o8=o8: e.tensor_copy(osb[:, o8, :], ps[:, obank, 0:130]),
                         reads=[("ps", obank)], writes=[("osb", o8)])
                    row0 = b * SEQ + il * NT + rr + r * nl * 128
                    P.dma(c.q2(), oscr_d.ap()[g, row0:row0 + r * 127 + 1:r, 0:130], osb[:, o8, :],
                          reads=[("osb", o8)], writes=[("oscr", b, g, qb)])

            emit_S(0, n)
            emit_S(1, n + 1)
            for s in range(32):
                if s + 2 < 32:
                    emit_S(s + 2, n + 2)
                emit_rest(s, n)
                n += 1
        if g < 2:
            continue
        for Tq in range(16):
            k2 = Tq % 2
            deps = []
            for T in range(4 * Tq, 4 * Tq + 4):
                il, Tl = T // 16, T % 16
                deps.append(("oscr", b, 0, T))
                deps += [("oscr", b, 1, il * 16 + rr * 4 + Tl // 4) for rr in range(4)]
                deps += [("oscr", b, 2, il * 16 + rr) for rr in range(16)]
            deps = sorted(set(deps))
            ocv = oc[:, k2 * 12:(k2 + 1) * 12, :].rearrange("p (j g) c -> p j g c", g=3)
            osv = osum[:, k2 * 4:(k2 + 1) * 4, :]
            for g in range(3):
                P.dma(c.q2(), ocv[:, :, g, :],
                      oscr_d.ap()[g, b * SEQ + Tq * 512:b * SEQ + (Tq + 1) * 512, 0:130].rearrange("(j p) c -> p j c", p=128),
                      reads=deps, writes=[("oc", k2, g)])
            P.op("dve", lambda e, ocv=ocv, osv=osv: e.tensor_tensor(osv, ocv[:, :, 0, :], ocv[:, :, 1, :], ALU.add),
                 reads=[("oc", k2, 0), ("oc", k2, 1)], writes=[("osum", k2), ("osum2", k2)])
            P.op("dve", lambda e, ocv=ocv, osv=osv: e.tensor_tensor(osv, osv, ocv[:, :, 2, :], ALU.add),
                 reads=[("oc", k2, 2), ("osum", k2)], writes=[("osum2", k2)])
            rv = rec[:, k2 * 4:(k2 + 1) * 4].unsqueeze(2)
            P.op("dve", lambda e, rv=rv, osv=osv: e.reciprocal(rv, osv[:, :, 128:129]),
                 reads=[("osum2", k2)], writes=[("rec", k2)])
            obv = ob[:, k2 * 4:(k2 + 1) * 4, :]
            P.op("dve", lambda e, rv=rv, osv=osv, obv=obv: e.tensor_tensor(obv, osv[:, :, 0:128], rv.to_broadcast([128, 4, 128]), ALU.mult),
                 reads=[("osum2", k2), ("rec", k2)], writes=[("ob", k2)])
            P.dma(c.q2(), ret_dst(b, 4 * Tq, 4), obv, reads=[("ob", k2)], writes=[("ret", b, Tq)])


def emit_attn_B(c, tok_src, ret_dst, mask_d, lam_d, subln_d, cst_d, nb=2):
    P, ar, ps = c.P, c.ar, c.ps
    c.stage_begin()
    toks = [[ar.alloc([128, 64, 128], BF16) for _ in range(3)] for _ in range(2)]
    Vaug = ar.alloc([128, 64, 130], BF16)
    QT = ar.alloc([128, SEQ], BF16)
    QT2 = ar.alloc([128, SEQ], BF16)
    KT = ar.alloc([128, SEQ], BF16)
    mask = ar.alloc([128, 512], BF16)
    PT = ar.alloc([128, 2 * 2, 512], BF16)
    lpb = ar.alloc([128, 256], F32)
    lpp = ar.alloc([128, 128], F32)
    lsum = ar.alloc([128, 2], F32)
    lam = ar.alloc([128, 4], F32)
    cst = ar.alloc([128, 2], F32)
    G2 = ar.alloc([128, 128], F32)
    rc = ar.alloc([128, 2 * 4], F32)
    o1 = ar.alloc([128, 2, 128], F32)
    av = ar.alloc([128, 2, 128], F32)
    sqj = ar.alloc([128, 2, 128], F32)
    ss2 = ar.alloc([128, 2], F32)
    rs2 = ar.alloc([128, 4], F32)
    rstd2 = ar.alloc([128, 4], F32)
    avs = ar.alloc([128, 4, 128], F32)
    pending = []
    ob = ar.alloc([128, 2 * 2, 128], BF16)
    scale = 64.0 ** -0.5
    P.dma("sp", mask, mask_d.ap(), writes=["mask"])
    P.dma("sp", lpb, lam_d.ap().rearrange("a b -> (a b)").partition_broadcast(128), writes=["lpb"])
    P.dma("sp", G2, subln_d.ap().partition_broadcast(128), writes=["G2raw"])
    P.dma("sp", cst, cst_d.ap(), writes=["cst"])
    P.op("pool", lambda e: e.memset(Vaug[:, :, 128:130], 1.0), writes=["Vones"])
    P.op("pool", lambda e: e.memset(QT[64:128, :], 0.0), writes=["qzero"])
    P.op("pool", lambda e: e.memset(QT2[0:64, :], 0.0), writes=["qzero"])
    P.op("dve", lambda e: e.tensor_tensor(lpp[:, 0:64], lpb[:, 0:64], lpb[:, 64:128], ALU.mult), reads=["lpb"], writes=["lpp0"])
    P.op("dve", lambda e: e.tensor_tensor(lpp[:, 64:128], lpb[:, 128:192], lpb[:, 192:256], ALU.mult), reads=["lpb"], writes=["lpp1"])
    P.op("dve", lambda e: e.tensor_reduce(lsum[:, 0:2], lpp.rearrange("p (a b) -> p a b", b=64), AX.X, ALU.add),
         reads=["lpp0", "lpp1"], writes=["lsum"])
    P.op("act", lambda e: e.activation(lam[:, 0:2], lsum[:, 0:2], AF.Exp), reads=["lsum"], writes=["lexp"])
    P.op("dve", lambda e: e.tensor_tensor(lam[:, 2:3], lam[:, 0:1], lam[:, 1:2], ALU.subtract), reads=["lexp"], writes=["ldiff"])
    P.op("dve", lambda e: e.tensor_scalar(lam[:, 3:4], lam[:, 2:3], cst[:, 0:1], -1.0, ALU.add, ALU.mult),
         reads=["ldiff", "cst"], writes=["nlam"])
    P.op("dve", lambda e: e.tensor_scalar(G2, G2, cst[:, 1:2], None, ALU.mult), reads=["G2raw", "cst"], writes=["G2"])
    gstep = 0

    def loads(u):
        for w3 in range(3):
            _load_tok(c, toks[u % 2][w3], lambda qi, w3=w3: tok_src(u, w3, qi), ("tok", u % 2, w3))

    loads(0)
    for b in range(nb):
        if b + 1 < nb:
            loads(b + 1)
        _copy_v(c, toks[b % 2][2], Vaug, ("tok", b % 2, 2))
        _transpose_tok(c, toks[b % 2][0], QT, ("tok", b % 2, 0), "QT", 0, dstT2=QT2)
        _transpose_tok(c, toks[b % 2][1], KT, ("tok", b % 2, 1), "KT", 0)
        steps = [(qc, kp) for qc in range(32) for kp in range(qc + 1)]

        def emit_S(i):
            qc, kp = steps[i]
            bufi = (gstep + i) % 2
            q0 = qc * 256
            for kbi in range(2):
                kb = 2 * kp + kbi
                for comp in range(2):
                    bank = bufi * 2 + comp
                    lo, hi = comp * 64, comp * 64 + 64
                    if DBG.get("noS") and i > 3:
                        continue
                    qsrc = QT if comp == 0 else QT2
                    P.op("pe", lambda e, bank=bank, kbi=kbi, kb=kb, qsrc=qsrc, q0=q0:
                         e.matmul(ps[:, bank, kbi * 256:(kbi + 1) * 256], KT[:, kb * 128:(kb + 1) * 128],
                                  qsrc[:, q0:q0 + 256], start=True, stop=True),
                         reads=[("KT", kb // 8), ("QT" if comp == 0 else "QT2", q0 // 1024), "qzero"], writes=[("ps", bank)])

        def emit_rest(i):
            qc, kp = steps[i]
            bufi = (gstep + i) % 2
            diag = (kp == qc)
            P.op("act", lambda e, bufi=bufi: e.activation(PT[:, bufi * 2:bufi * 2 + 2, :], ps[:, bufi * 2:bufi * 2 + 2, :], AF.Exp, scale=scale),
                 reads=[("ps", bufi * 2), ("ps", bufi * 2 + 1)],
                 writes=[("PT", bufi, 0), ("PTa", bufi, 0), ("PT", bufi, 1), ("PTa", bufi, 1)])
            while pending:
                pending.pop(0)()
            for comp in range(2):
                if diag:
                    P.op("dve", lambda e, bufi=bufi, comp=comp: e.tensor_tensor(
                        PT[:, bufi * 2 + comp, :].rearrange("p (a b) -> p a b", b=128)[:, 0:4:3, :],
                        PT[:, bufi * 2 + comp, :].rearrange("p (a b) -> p a b", b=128)[:, 0:4:3, :],
                        mask[:, 0:128].unsqueeze(1).to_broadcast([128, 2, 128]), ALU.mult),
                        reads=[("PT", bufi, comp), "mask"], writes=[("PTa", bufi, comp)])
            for kbi in range(2):
                kb = 2 * kp + kbi
                for j in range(2):
                    if diag and kbi == 1 and j == 0:
                        continue
                    first = (kp == 0 and kbi == 0)
                    last = (diag and kbi == j)
                    for comp in range(2):
                        obank = 4 + j * 2 + comp
                        if DBG.get("noPV") and not (first or last):
                            continue
                        P.op("pe", lambda e, obank=obank, bufi=bufi, comp=comp, kbi=kbi, j=j, kb=kb, first=first, last=last:
                             e.matmul(ps[:, obank, 0:130], PT[:, bufi * 2 + comp, kbi * 256 + j * 128:kbi * 256 + (j + 1) * 128],
                                      Vaug[:, kb, :], start=first, stop=last),
                             reads=[("PTa", bufi, comp), ("V", kb // 32), "Vones"], writes=[("ps", obank)])
            if not diag:
                return
            e2 = qc % 2
            for j in range(2):
                b1, b2 = 4 + j * 2, 4 + j * 2 + 1
                P.op("dve", lambda e, j=j, b1=b1: e.reciprocal(rc[:, j * 4:j * 4 + 1], ps[:, b1, 128:129]),
                     reads=[("ps", b1)], writes=[("rc1", j)])
                P.op("dve", lambda e, j=j, b2=b2: e.reciprocal(rc[:, j * 4 + 1:j * 4 + 2], ps[:, b2, 128:129]),
                     reads=[("ps", b2)], writes=[("rc2", j)])
                P.op("dve", lambda e, j=j: e.tensor_tensor(rc[:, j * 4 + 2:j * 4 + 3], rc[:, j * 4 + 1:j * 4 + 2], lam[:, 3:4], ALU.mult),
                     reads=[("rc2", j), "nlam"], writes=[("rc3", j)])
                P.op("dve", lambda e, j=j, b1=b1: e.tensor_scalar(o1[:, j, :], ps[:, b1, 0:128], rc[:, j * 4:j * 4 + 1], None, ALU.mult),
                     reads=[("ps", b1), ("rc1", j)], writes=[("o1", j)])
                P.op("dve", lambda e, j=j, b2=b2: e.scalar_tensor_tensor(av[:, j, :], ps[:, b2, 0:128], rc[:, j * 4 + 2:j * 4 + 3], o1[:, j, :], ALU.mult, ALU.add),
                     reads=[("ps", b2), ("rc3", j), ("o1", j)], writes=[("av", j)])
                P.op("pool", lambda e, j=j: e.tensor_tensor(sqj[:, j, :], av[:, j, :], av[:, j, :], ALU.mult),
                     reads=[("av", j)], writes=[("sqj", j)])
                P.op("dve", lambda e, j=j: e.tensor_reduce(ss2[:, j:j + 1], sqj[:, j, :], AX.X, ALU.add),
                     reads=[("sqj", j)], writes=[("ss2", j)])
                P.op("dve", lambda e, j=j, e2=e2: e.tensor_scalar(rs2[:, e2 * 2 + j:e2 * 2 + j + 1], ss2[:, j:j + 1], 1.0 / 128.0, EPS, ALU.mult, ALU.add),
                     reads=[("ss2", j)], writes=[("rs2", e2, j)])
                P.op("pool", lambda e, j=j, e2=e2: e.tensor_copy(avs[:, e2 * 2 + j, :], av[:, j, :]),
                     reads=[("av", j)], writes=[("avs", e2, j)])

            def phase_b(qc=qc, e2=e2, b=b):
                P.op("act", lambda e: e.activation(rs2[:, e2 * 2:e2 * 2 + 2], rs2[:, e2 * 2:e2 * 2 + 2], AF.Sqrt),
                     reads=[("rs2", e2, 0), ("rs2", e2, 1)], writes=[("rs2b", e2)])
                P.op("dve", lambda e: e.reciprocal(rstd2[:, e2 * 2:e2 * 2 + 2], rs2[:, e2 * 2:e2 * 2 + 2]),
                     reads=[("rs2b", e2)], writes=[("rstd2", e2)])
                for j in range(2):
                    P.op("dve", lambda e, j=j: e.scalar_tensor_tensor(ob[:, e2 * 2 + j, :], avs[:, e2 * 2 + j, :], rstd2[:, e2 * 2 + j:e2 * 2 + j + 1], G2, ALU.mult, ALU.mult),
                         reads=[("avs", e2, j), ("rstd2", e2), "G2"], writes=[("ob", e2, j)])
                P.dma("sp", ret_dst(b, 2 * qc, 2), ob[:, e2 * 2:e2 * 2 + 2, :],
                      reads=[("ob", e2, 0), ("ob", e2, 1)], writes=[("ret", b, qc)])

            pending.append(phase_b)

        emit_S(0)
        for i in range(len(steps)):
            if i + 1 < len(steps):
                emit_S(i + 1)
            emit_rest(i)
        while pending:
            pending.pop(0)()
        gstep += len(steps)


def emit_out(c, retrecv_d, zs_d, wo_d):
    P, ar, ps = c.P, c.ar, c.ps
    c.stage_begin()
    X = c.X
    wst = ar.alloc([128, 8, DM], F32)
    wo = ar.alloc([128, 8, DM], BF16)
    oc = ar.alloc([128, 2, DM], BF16)
    zt = ar.alloc([128, 2, DM], BF16)
    y = ar.alloc([128, 2, DM], BF16)
    yT = ar.alloc([128, 2 * 8, 128], BF16)
    wv = wo_d.ap().rearrange("(c p) n -> p c n", p=128)
    for h2 in range(4):
        P.dma(c.q(), wst[:, h2 * 2:h2 * 2 + 2, :], wv[:, h2 * 2:h2 * 2 + 2, :], writes=[("wst", h2)])
    for cc in range(8):
        if cc % 2 == 0:
            P.op("act", lambda e, cc=cc: e.activation(wo[:, cc, :], wst[:, cc, :], AF.Copy), reads=[("wst", cc // 2)], writes=[("wo", cc)])
        else:
            P.op("pool", lambda e, cc=cc: e.tensor_copy(wo[:, cc, :], wst[:, cc, :]), reads=[("wst", cc // 2)], writes=[("wo", cc)])
    for t in range(NTILE):
        k2 = t % 2
        P.dma("sp", oc[:, k2, :], retrecv_d.ap()[t * 128:(t + 1) * 128, :], writes=[("oc", k2)])
        P.dma("act", zt[:, k2, :], zs_d.ap()[t * 128:(t + 1) * 128, :], writes=[("zt", k2)])
        P.op("dve", lambda e, k2=k2: e.tensor_tensor(y[:, k2, :], oc[:, k2, :], zt[:, k2, :], ALU.mult),
             reads=[("oc", k2), ("zt", k2)], writes=[("y", k2)])
        bank = k2
        tp = ps[:, bank, :].bitcast(BF16)
        for cc in range(8):
            P.op("pe", lambda e, tp=tp, cc=cc, k2=k2: e.transpose(tp[:, cc * 128:(cc + 1) * 128], y[:, k2, cc * 128:(cc + 1) * 128], c.ident),
                 reads=[("y", k2), "ident"], writes=[("ps", bank)])
        P.op("act", lambda e, tp=tp, k2=k2: e.activation(yT[:, k2 * 8:(k2 + 1) * 8, :], tp.rearrange("p (c k) -> p c k", k=128), AF.Copy),
             reads=[("ps", bank)], writes=[("yT", k2)])
        for hf in range(2):
            bank2 = 2 + (t * 2 + hf) % 4
            for cc in range(8):
                P.op("pe", lambda e, bank2=bank2, cc=cc, k2=k2, hf=hf: e.matmul(ps[:, bank2, :], yT[:, k2 * 8 + cc, :], wo[:, cc, hf * 512:(hf + 1) * 512],
                                                                              start=(cc == 0), stop=(cc == 7)),
                     reads=[("yT", k2), ("wo", cc)], writes=[("ps", bank2)])
            P.op("dve", lambda e, bank2=bank2, t=t, hf=hf: e.tensor_tensor(X[:, t, hf * 512:(hf + 1) * 512], X[:, t, hf * 512:(hf + 1) * 512], ps[:, bank2, :], ALU.add),
                 reads=[("ps", bank2), ("X", t)], writes=[("X", t)])


def emit_store_x(c, xo_d):
    xv = xo_d.ap().rearrange("(t p) f -> p t f", p=128)
    for k in range(4):
        c.P.dma(c.q(), xv[:, 4 * k:4 * k + 4, :], c.X[:, 4 * k:4 * k + 4, :],
                reads=[("X", t) for t in range(4 * k, 4 * k + 4)], writes=[("xo", k)])


def emit_final_norm(c, gf_d, y_d):
    P, ar = c.P, c.ar
    c.stage_begin()
    X = c.X
    GF = ar.alloc([128, DM], F32)
    sq = ar.alloc([128, 2, DM], BF16)
    ss = ar.alloc([128, NTILE], F32)
    rs = ar.alloc([128, NTILE], F32)
    rstd = ar.alloc([128, NTILE], F32)
    yo = ar.alloc([128, 2, DM], F32)
    P.dma("sp", GF, gf_d.ap().partition_broadcast(128), writes=["GF"])
    for t in range(NTILE):
        P.op("act", lambda e, t=t: e.activation(sq[:, t % 2, :], X[:, t, :], AF.Square, accum_out=ss[:, t:t + 1]),
             reads=[("X", t)], writes=[("ss", t), ("sq", t % 2)])
    emit_rstd(c, ss, rs, rstd, NTILE, DM, [("ss", t) for t in range(NTILE)])
    yv = y_d.ap().rearrange("(t p) f -> p t f", p=128)
    for t in range(NTILE):
        k2 = t % 2
        P.op("dve", lambda e, t=t, k2=k2: e.scalar_tensor_tensor(yo[:, k2, :], X[:, t, :], rstd[:, t:t + 1], GF, ALU.mult, ALU.mult),
             reads=[("X", t), "rstd", "GF"], writes=[("yo", k2)])
        P.dma(c.q(), yv[:, t, :], yo[:, k2, :], reads=[("yo", k2)], writes=[("y", t)])


def build_program(stages):
    nc = bass.Bass("TRN2", target_bir_lowering=False)
    c = Ctx(nc, need_x=any(st in ("load_x", "out", "final", "projA", "projB") for st in stages))
    EI, EO = "ExternalInput", "ExternalOutput"
    ident_d = c.dram("ident", [128, 128], BF16, EI)
    emit_consts(c, ident_d)
    for s in stages:
        if s == "load_x":
            emit_load_x(c, c.dram("x_in", [NT, DM], F32, EI))
        elif s == "store_x":
            emit_store_x(c, c.dram("x_out", [NT, DM], F32, EO))
        elif s in ("projA", "projB"):
            kind = s[-1]
            ncols = A_IN if kind == "A" else B_IN
            half = 16 if kind == "A" else 8
            ngr = 3 if kind == "A" else 1
            nslot = 9 if kind == "A" else 3
            emit_proj(c, kind,
                      c.dram("w_in", [DM, ncols], F32, EI),
                      c.dram("g_in", [128, 8], F32, EI),
                      c.dram("rope", [128, ngr * NTILE * 2, half], F32, EI),
                      c.dram("send", [nslot, NT, DM], BF16, EO),
                      c.dram("zs_out", [NT, DM], BF16, EO))
        elif s == "attnA":
            recv_d = c.dram("recv", [2, 9, 128, 64, 128], BF16, EI)
            ret_d = c.dram("ret", [2, 128, 64, 128], BF16, EO)
            emit_attn_A(c, lambda b, slot, qi, r=recv_d: r.ap()[b, slot, :, qi * 16:(qi + 1) * 16, :],
                        lambda b, T0, n, r=ret_d: r.ap()[b, :, T0:T0 + n, :],
                        c.dram("oscr", [3, 2 * SEQ, 132], F32, "Internal"),
                        c.dram("mask", [128, 512], BF16, EI))
        elif s == "attnB":
            recv_d = c.dram("recv", [2, 3, 128, 64, 128], BF16, EI)
            ret_d = c.dram("ret", [2, 128, 64, 128], BF16, EO)
            emit_attn_B(c, lambda b, slot, qi, r=recv_d: r.ap()[b, slot, :, qi * 16:(qi + 1) * 16, :],
                        lambda b, T0, n, r=ret_d: r.ap()[b, :, T0:T0 + n, :],
                        c.dram("mask", [128, 512], BF16, EI),
                        c.dram("lam", [4, 64], F32, EI),
                        c.dram("subln", [128], F32, EI),
                        c.dram("cst", [128, 2], F32, EI))
        elif s == "out":
            emit_out(c, c.dram("retrecv", [NT, DM], BF16, EI),
                     c.dram("zs_in", [NT, DM], BF16, EI),
                     c.dram("w_out", [DM, DM], F32, EI))
        elif s == "final":
            emit_final_norm(c, c.dram("gf", [DM], F32, EI), c.dram("y", [NT, DM], F32, EO))
        else:
            raise ValueError(s)
    c.P.finish()
    return nc


def _rope_table(kind, core):
    pos0 = (core % 4) * NT
    p = np.arange(128)
    if kind == "A":
        rot = 32
        inv = (1.0 / (ROPE_THETA ** (np.arange(0, rot, 2, dtype=np.float32) / rot))).astype(np.float32)
        out = np.zeros((128, 3 * NTILE * 2, 16), np.float32)
        for g, r in enumerate(DIL):
            per = NTILE // r
            for tt in range(NTILE):
                rr, nl = tt // per, tt % per
                pos = (pos0 + rr + r * (nl * 128 + p)).astype(np.float32)
                ang = pos[:, None] * inv[None, :]
                out[:, (g * NTILE + tt) * 2, :] = np.cos(ang)
                out[:, (g * NTILE + tt) * 2 + 1, :] = np.sin(ang)
        return out
    rot = 16
    inv = (1.0 / (ROPE_THETA ** (np.arange(0, rot, 2, dtype=np.float32) / rot))).astype(np.float32)
    out = np.zeros((128, NTILE * 2, 8), np.float32)
    for tt in range(NTILE):
        pos = (pos0 + tt * 128 + p).astype(np.float32)
        ang = pos[:, None] * inv[None, :]
        out[:, tt * 2, :] = np.cos(ang)
        out[:, tt * 2 + 1, :] = np.sin(ang)
    return out


_BF = ml_dtypes.bfloat16
_PROGS = {}


def _prog(stages):
    key = tuple(stages)
    if key not in _PROGS:
        _PROGS[key] = build_program(stages)
    return _PROGS[key]


def _run(stages, in_maps):
    nc = _prog(stages)
    res = run_bass_kernel_spmd(nc, in_maps, core_ids=list(range(NCORES)))
    return res.results


def _to_heads(send, nslot):
    S = np.stack(send).reshape(2, 4, nslot, 16, 128, 8, 128)
    S = S.transpose(5, 0, 2, 4, 1, 3, 6)
    return [np.ascontiguousarray(S[h]).reshape(2, nslot, 128, 64, 128) for h in range(NCORES)]


def _to_tokens(ret):
    R = np.stack(ret).reshape(8, 2, 128, 4, 16, 128)
    R = R.transpose(1, 3, 4, 2, 0, 5)
    R = np.ascontiguousarray(R).reshape(NCORES, NT, DM)
    return [R[i] for i in range(NCORES)]


def _glayout(g):
    return np.ascontiguousarray(g.reshape(8, 128).T)


def kernel_unfused(x, a_norm, a_w_in, a_w_out, b_norm, b_w_in, b_lambda, b_subln, b_w_out, final_norm):
    import math
    x = np.asarray(x, np.float32)
    xs = [np.ascontiguousarray(s) for s in np.split(x.reshape(-1, DM), NCORES, axis=0)]
    ident = np.eye(128, dtype=np.float32).astype(_BF)
    kk = np.arange(128)[:, None]
    qq = np.arange(128)[None, :]
    mask = np.concatenate([(kk <= qq), (kk >= qq)] * 2, axis=1).astype(np.float32).astype(_BF)
    ropeA = [_rope_table("A", i) for i in range(NCORES)]
    ropeB = [_rope_table("B", i) for i in range(NCORES)]
    w_in = [np.asarray(a_w_in[0]), np.asarray(b_w_in[0]), np.asarray(a_w_in[1]), np.asarray(b_w_in[1])]
    w_out = [np.asarray(a_w_out[0]), np.asarray(b_w_out[0]), np.asarray(a_w_out[1]), np.asarray(b_w_out[1])]
    gs = [_glayout(np.asarray(a_norm[0])), _glayout(np.asarray(b_norm[0])),
          _glayout(np.asarray(a_norm[1])), _glayout(np.asarray(b_norm[1]))]
    kinds = ["A", "B", "A", "B"]

    def proj_inputs(layer, i):
        kd = kinds[layer]
        return {"w_in": np.ascontiguousarray(w_in[layer], dtype=np.float32), "g_in": gs[layer],
                "rope": ropeA[i] if kd == "A" else ropeB[i]}

    res = _run(["load_x", "projA"], [dict(ident=ident, x_in=xs[i], **proj_inputs(0, i)) for i in range(NCORES)])
    send = [r["send"] for r in res]
    zs = [r["zs_out"] for r in res]
    for layer in range(4):
        kd = kinds[layer]
        recv = _to_heads(send, 9 if kd == "A" else 3)
        if kd == "A":
            res = _run(["attnA"], [dict(ident=ident, recv=recv[h], mask=mask) for h in range(NCORES)])
        else:
            j = layer // 2
            lam_init = 0.8 - 0.6 * math.exp(-0.3 * layer)
            cst = np.tile(np.array([[lam_init, 1.0 - lam_init]], np.float32), (128, 1))
            res = _run(["attnB"], [dict(ident=ident, recv=recv[h], mask=mask,
                                       lam=np.ascontiguousarray(b_lambda[j], dtype=np.float32),
                                       subln=np.ascontiguousarray(b_subln[j], dtype=np.float32), cst=cst)
                                  for h in range(NCORES)])
        retrecv = _to_tokens([r["ret"] for r in res])
        wo = np.ascontiguousarray(w_out[layer], dtype=np.float32)
        if layer < 3:
            nk = kinds[layer + 1]
            res = _run(["load_x", "out", "proj" + nk, "store_x"],
                       [dict(ident=ident, x_in=xs[i], retrecv=retrecv[i], zs_in=zs[i], w_out=wo,
                             **proj_inputs(layer + 1, i)) for i in range(NCORES)])
            xs = [r["x_out"] for r in res]
            send = [r["send"] for r in res]
            zs = [r["zs_out"] for r in res]
        else:
            res = _run(["load_x", "out", "final"],
                       [dict(ident=ident, x_in=xs[i], retrecv=retrecv[i], zs_in=zs[i], w_out=wo,
                             gf=np.ascontiguousarray(final_norm, dtype=np.float32)) for i in range(NCORES)])
            ys = [r["y"] for r in res]
    return np.concatenate(ys, axis=0).reshape(2, SEQ, DM).astype(np.float32)


class _V:
    def __init__(self, ap):
        self._ap = ap

    def ap(self):
        return self._ap


def _x_scope_begin(c):
    c.P.barrier()
    c.base = c.base0
    c.ar.reset(c.base0)
    c.X = c.ar.alloc([128, NTILE, DM], F32)
    c.base = c.ar.mark()


def _x_scope_end(c):
    c.base = c.base0
    c.X = None


def build_fused():
    nc = bass.Bass("TRN2", target_bir_lowering=False)
    c = Ctx(nc, need_x=False)
    c.base0 = c.base
    EI, EO, IN = "ExternalInput", "ExternalOutput", "Internal"
    kinds = ["A", "B", "A", "B"]
    ident_d = c.dram("ident", [128, 128], BF16, EI)
    x_in = c.dram("x_in", [4, NT, DM], F32, EI)
    y_d = c.dram("y", [4, NT, DM], F32, EO)
    w_in = [c.dram("w_in%d" % l, [DM, A_IN if kinds[l] == "A" else B_IN], F32, EI) for l in range(4)]
    g_in = [c.dram("g_in%d" % l, [128, 8], F32, EI) for l in range(4)]
    w_out = [c.dram("w_out%d" % l, [DM, DM], F32, EI) for l in range(4)]
    ropeA = c.dram("ropeA", [4, 128, 3 * NTILE * 2, 16], F32, EI)
    ropeB = c.dram("ropeB", [4, 128, NTILE * 2, 8], F32, EI)
    mask_d = c.dram("mask", [128, 512], BF16, EI)
    lam_d = [c.dram("lam%d" % j, [4, 64], F32, EI) for j in range(2)]
    subln_d = [c.dram("subln%d" % j, [128], F32, EI) for j in range(2)]
    cst_d = [c.dram("cst%d" % j, [128, 2], F32, EI) for j in range(2)]
    gf_d = c.dram("gf", [DM], F32, EI)
    xs = c.dram("xs", [4, NT, DM], F32, IN)
    send = c.dram("send", [4, 9, NT, DM], BF16, IN)
    zs = c.dram("zs", [4, NT, DM], BF16, IN)
    retr = c.dram("retr", [4, NT, DM], BF16, IN)
    oscr = c.dram("oscr", [3, SEQ, 132], F32, IN)
    emit_consts(c, ident_d)

    def proj(layer, il):
        kd = kinds[layer]
        rope = ropeA if kd == "A" else ropeB
        emit_proj(c, kd, w_in[layer], g_in[layer], _V(rope.ap()[il]), _V(send.ap()[il]), _V(zs.ap()[il]))

    for il in range(4):
        _x_scope_begin(c)
        emit_load_x(c, _V(x_in.ap()[il]))
        proj(0, il)
    for layer in range(4):
        _x_scope_end(c)
        for h in range(8):
            def tok_src(b, slot, qi, h=h):
                return send.ap()[qi, slot, :, h * 128:(h + 1) * 128].rearrange("(k p) d -> p k d", p=128)

            def ret_dst(b, T0, n, h=h):
                il, Tl = T0 // 16, T0 % 16
                return retr.ap()[il, Tl * 128:(Tl + n) * 128, h * 128:(h + 1) * 128].rearrange("(j p) d -> p j d", p=128)

            if kinds[layer] == "A":
                emit_attn_A(c, tok_src, ret_dst, oscr, mask_d, nb=1)
            else:
                j = layer // 2
                emit_attn_B(c, tok_src, ret_dst, mask_d, lam_d[j], subln_d[j], cst_d[j], nb=1)
        for il in range(4):
            _x_scope_begin(c)
            emit_load_x(c, _V(x_in.ap()[il] if layer == 0 else xs.ap()[il]))
            emit_out(c, _V(retr.ap()[il]), _V(zs.ap()[il]), w_out[layer])
            if layer < 3:
                proj(layer + 1, il)
                emit_store_x(c, _V(xs.ap()[il]))
            else:
                emit_final_norm(c, gf_d, _V(y_d.ap()[il]))
    c.P.finish()
    return nc


def kernel_fused(x, a_norm, a_w_in, a_w_out, b_norm, b_w_in, b_lambda, b_subln, b_w_out, final_norm):
    import math
    x = np.asarray(x, np.float32)
    ident = np.eye(128, dtype=np.float32).astype(_BF)
    kk = np.arange(128)[:, None]
    qq = np.arange(128)[None, :]
    mask = np.concatenate([(kk <= qq), (kk >= qq)] * 2, axis=1).astype(np.float32).astype(_BF)
    ropeA = np.stack([_rope_table("A", i) for i in range(4)])
    ropeB = np.stack([_rope_table("B", i) for i in range(4)])
    f32 = lambda a: np.ascontiguousarray(np.asarray(a), dtype=np.float32)
    common = dict(ident=ident, mask=mask, ropeA=ropeA, ropeB=ropeB, gf=f32(final_norm),
                  w_in0=f32(a_w_in[0]), w_in1=f32(b_w_in[0]), w_in2=f32(a_w_in[1]), w_in3=f32(b_w_in[1]),
                  w_out0=f32(a_w_out[0]), w_out1=f32(b_w_out[0]), w_out2=f32(a_w_out[1]), w_out3=f32(b_w_out[1]),
                  g_in0=_glayout(np.asarray(a_norm[0])), g_in1=_glayout(np.asarray(b_norm[0])),
                  g_in2=_glayout(np.asarray(a_norm[1])), g_in3=_glayout(np.asarray(b_norm[1])),
                  lam0=f32(b_lambda[0]), lam1=f32(b_lambda[1]), subln0=f32(b_subln[0]), subln1=f32(b_subln[1]))
    for j, layer in enumerate((1, 3)):
        lam_init = 0.8 - 0.6 * math.exp(-0.3 * layer)
        common["cst%d" % j] = np.tile(np.array([[lam_init, 1.0 - lam_init]], np.float32), (128, 1))
    in_maps = [dict(common, x_in=np.ascontiguousarray(x[c % 2].reshape(4, NT, DM))) for c in range(NCORES)]
    if "fused" not in _PROGS:
        _PROGS["fused"] = build_fused()
    res = run_bass_kernel_spmd(_PROGS["fused"], in_maps, core_ids=list(range(NCORES))).results
    return np.stack([res[0]["y"].reshape(SEQ, DM), res[1]["y"].reshape(SEQ, DM)]).astype(np.float32)


MODE = "fused"


def kernel(**inputs):
    if MODE == "fused":
        return kernel_fused(**inputs)
    return kernel_unfused(**inputs)
```

```python
import contextlib
import numpy as np
import ml_dtypes
import concourse.bass as bass
import concourse.mybir as mybir
from concourse.bass_utils import run_bass_kernel_spmd

F32 = mybir.dt.float32
BF16 = mybir.dt.bfloat16
AF = mybir.ActivationFunctionType
ALU = mybir.AluOpType
AX = mybir.AxisListType


class _Op:
    __slots__ = ("eng", "fn", "reads", "writes", "dma", "signal", "sigval", "sem", "waits", "barrier")

    def __init__(self, eng, fn, reads, writes, dma=False, barrier=False):
        self.eng = eng
        self.fn = fn
        self.reads = reads
        self.writes = writes
        self.dma = dma
        self.signal = False
        self.sigval = 0
        self.sem = None
        self.waits = []
        self.barrier = barrier


class Prog:
    ENGS = ("pe", "act", "dve", "pool", "sp")

    def __init__(self, nc, n_dma_sems=6):
        self.nc = nc
        self.ops = []
        self.stack = contextlib.ExitStack()
        self.n_dma_sems = n_dma_sems
        self._uid = 0

    def sb(self, name, shape, dtype):
        return self.stack.enter_context(self.nc.sbuf_tensor(name, list(shape), dtype))

    def ps(self, name, shape, dtype):
        return self.stack.enter_context(self.nc.psum_tensor(name, list(shape), dtype))

    def op(self, eng, fn, reads=(), writes=()):
        self.ops.append(_Op(eng, fn, tuple(reads), tuple(writes)))

    def dma(self, q, out, in_, reads=(), writes=(), **kw):
        self.ops.append(_Op(q, lambda e: e.dma_start(out=out, in_=in_, **kw), tuple(reads), tuple(writes), dma=True))

    def barrier(self):
        self.ops.append(_Op(None, None, (), (), barrier=True))

    def finish(self):
        nc = self.nc
        self.barrier()
        ops = self.ops
        last_writer = {}
        readers = {}
        deps = [None] * len(ops)
        for j, o in enumerate(ops):
            if o.barrier:
                continue
            d = set()
            xr = tuple(t for t in o.reads if isinstance(t, tuple) and t[0] == "ps")
            if xr:
                o.reads = tuple(t for t in o.reads if t not in xr)
                o.writes = tuple(o.writes) + tuple(t for t in xr if t not in o.writes)
            for t in o.reads:
                w = last_writer.get(t)
                if w is not None:
                    d.add(w)
            for t in o.writes:
                w = last_writer.get(t)
                if w is not None:
                    d.add(w)
                for r in readers.get(t, ()):
                    d.add(r)
            for t in o.writes:
                last_writer[t] = j
                readers[t] = []
            for t in o.reads:
                readers.setdefault(t, []).append(j)
            d.discard(j)
            keep = []
            for i in d:
                p = ops[i]
                if not p.dma and not o.dma and p.eng == o.eng:
                    if o.eng in ("pe", "sp"):
                        continue
                keep.append(i)
            best = {}
            kk = []
            for i in keep:
                p = ops[i]
                if p.dma:
                    kk.append(i)
                elif i > best.get(p.eng, -1):
                    best[p.eng] = i
            keep = kk + list(best.values())
            deps[j] = keep
            for i in keep:
                ops[i].signal = True
        last_on = {}
        for j, o in enumerate(ops):
            if o.barrier:
                for e, i in last_on.items():
                    ops[i].signal = True
            elif not o.dma:
                last_on[o.eng] = j
        sems = {}
        for e in self.ENGS:
            sems[e] = self.stack.enter_context(nc.semaphore("s_" + e))
        dma_sems = {}
        for q in ("sp", "act", "pool"):
            dma_sems[q] = [self.stack.enter_context(nc.semaphore("d_%s%d" % (q, k))) for k in range(self.n_dma_sems)]
        cnt = {e: 0 for e in self.ENGS}
        dcnt = {q: 0 for q in dma_sems}
        dtot = {}
        prev_dma_wait = {}
        for j, o in enumerate(ops):
            if o.barrier:
                continue
            if o.dma:
                k = dcnt[o.eng] % self.n_dma_sems
                dcnt[o.eng] += 1
                s = dma_sems[o.eng][k]
                prev = dtot.get((o.eng, k), 0)
                if prev:
                    prev_dma_wait[j] = (s, prev)
                dtot[(o.eng, k)] = prev + 16
                o.sem = s
                o.sigval = prev + 16
            elif o.signal:
                cnt[o.eng] += 1
                o.sem = sems[o.eng]
                o.sigval = cnt[o.eng]
        known = {e: {} for e in self.ENGS}
        cur_sig = {e: 0 for e in self.ENGS}
        cur_dma = {}
        per_eng = {e: [] for e in self.ENGS}
        for j, o in enumerate(ops):
            if o.barrier:
                for e in self.ENGS:
                    w = []
                    for e2 in self.ENGS:
                        if e2 != e and cur_sig[e2] > known[e].get(id(sems[e2]), 0):
                            w.append((sems[e2], cur_sig[e2]))
                            known[e][id(sems[e2])] = cur_sig[e2]
                    for (s, v) in cur_dma.values():
                        if v > known[e].get(id(s), 0):
                            w.append((s, v))
                            known[e][id(s)] = v
                    if w:
                        per_eng[e].append((w, None))
                continue
            need = {}
            for i in deps[j]:
                p = ops[i]
                key = id(p.sem)
                if p.sigval > need.get(key, (None, 0))[1]:
                    need[key] = (p.sem, p.sigval)
            if j in prev_dma_wait:
                s, v = prev_dma_wait[j]
                if v > need.get(id(s), (None, 0))[1]:
                    need[id(s)] = (s, v)
            w = []
            for key, (s, v) in need.items():
                if v > known[o.eng].get(key, 0):
                    w.append((s, v))
                    known[o.eng][key] = v
            per_eng[o.eng].append((w, o))
            if o.dma:
                cur_dma[id(o.sem)] = (o.sem, o.sigval)
            elif o.signal:
                cur_sig[o.eng] = o.sigval
        engobj = {"pe": "tensor", "act": "scalar", "dve": "vector", "pool": "gpsimd", "sp": "sync"}
        block = self.stack.enter_context(nc.Block())

        def make(e):
            def body(eng):
                for w, o in per_eng[e]:
                    for (s, v) in w:
                        eng.wait_ge(s, v)
                    if o is None:
                        continue
                    ins = o.fn(eng)
                    if o.dma:
                        ins.then_inc(o.sem, 16)
                    elif o.signal:
                        ins.then_inc(o.sem, 1)
            return body

        for e in self.ENGS:
            getattr(block, engobj[e])(make(e))
        self.n_ops = len(ops)
        self.stack.close()


NCORES = 8
NT = 2048
NTILE = 16
DM = 1024
SEQ = 8192
EPS = 1e-6
ROPE_THETA = 500000.0
DIL = (1, 4, 16)
A_IN = 10240
B_IN = 4096


DBG = {}


class Arena:
    def __init__(self, P, nbytes):
        self.t = P.sb("arena", [128, nbytes // 2], BF16)
        self.n = nbytes // 2
        self.off = 0

    def mark(self):
        return self.off

    def reset(self, m):
        self.off = m

    def alloc(self, shape, dtype):
        esz = 4 if dtype == F32 else 2
        n = 1
        for s in shape[1:]:
            n *= s
        nb16 = n * esz // 2
        self.off = (self.off + 15) // 16 * 16
        assert self.off + nb16 <= self.n, "arena overflow %d + %d > %d" % (self.off, nb16, self.n)
        ap = self.t[0:shape[0], self.off:self.off + nb16]
        self.off += nb16
        if dtype == F32:
            ap = ap.bitcast(F32)
        if len(shape) == 3:
            ap = ap.rearrange("p (a b) -> p a b", b=shape[2])
        elif len(shape) == 4:
            ap = ap.rearrange("p (a b c) -> p a b c", b=shape[2], c=shape[3])
        return ap


class Ctx:
    def __init__(self, nc, need_x=True):
        self.nc = nc
        self.P = Prog(nc)
        self.ar = Arena(self.P, 204 * 1024)
        self.ps = self.P.ps("ps", [128, 8, 512], F32)
        self.X = self.ar.alloc([128, NTILE, DM], F32) if need_x else None
        self.ident = self.ar.alloc([128, 128], BF16)
        self.base = self.ar.mark()
        self.uid = 0
        self.dq = 0

    def q(self):
        return "sp"

    def q2(self):
        return "sp"

    def stage_begin(self):
        self.P.barrier()
        self.ar.reset(self.base)

    def dram(self, name, shape, dtype, kind):
        return self.nc.dram_tensor(name, list(shape), dtype, kind=kind)


def emit_consts(c, ident_d):
    c.P.dma("sp", c.ident, ident_d.ap(), writes=["ident"])


def emit_load_x(c, x_d):
    xv = x_d.ap().rearrange("(t p) f -> p t f", p=128)
    for k in range(4):
        c.P.dma(c.q(), c.X[:, 4 * k:4 * k + 4, :], xv[:, 4 * k:4 * k + 4, :],
                writes=[("X", t) for t in range(4 * k, 4 * k + 4)])


def emit_rstd(c, ss, rs, rstd, n, denom, reads):
    P = c.P
    P.op("dve", lambda e: e.tensor_scalar(rs, ss, 1.0 / denom, EPS, ALU.mult, ALU.add), reads=reads, writes=["rs"])
    P.op("act", lambda e: e.activation(rs, rs, AF.Sqrt), reads=["rs"], writes=["rs2"])
    P.op("dve", lambda e: e.reciprocal(rstd, rs), reads=["rs2"], writes=["rstd"])


def emit_proj(c, kind, w_d, g_d, rope_d, send_d, zs_d):
    P, ar, ps = c.P, c.ar, c.ps
    c.stage_begin()
    ncols = A_IN if kind == "A" else B_IN
    nblk = ncols // 512
    if kind == "A":
        nh, hd, half = 4, 128, 16
    else:
        nh, hd, half = 8, 64, 8
    hnT = ar.alloc([128, 8, NT], BF16)
    sq = ar.alloc([128, 2, DM], BF16)
    hn = ar.alloc([128, 2, DM], BF16)
    ss = ar.alloc([128, NTILE], F32)
    rs = ar.alloc([128, NTILE], F32)
    rstd = ar.alloc([128, NTILE], F32)
    G = ar.alloc([128, 8], F32)
    ngr = 3 if kind == "A" else 1
    rope = ar.alloc([128, ngr * NTILE * 2, half], F32)
    wst = ar.alloc([128, 2 * 8, 512], F32)
    wb = ar.alloc([128, 2 * 8, 512], BF16)
    st = ar.alloc([128, 4, 512], BF16)
    tmp = ar.alloc([128, 4 * 4, nh * half], F32)
    X, ident = c.X, c.ident

    P.dma("sp", G, g_d.ap(), writes=["G"])
    P.dma("sp", rope, rope_d.ap(), writes=["rope"])
    for t in range(NTILE):
        P.op("act", lambda e, t=t: e.activation(sq[:, t % 2, :], X[:, t, :], AF.Square, accum_out=ss[:, t:t + 1]),
             reads=[("X", t)], writes=[("ss", t), ("sq", t % 2)])
    lvl = DBG.get("lvl", 9)
    if lvl >= 2:
        emit_rstd(c, ss, rs, rstd, NTILE, DM, [("ss", t) for t in range(NTILE)])
    for t in range(NTILE if lvl >= 3 else 0):
        P.op("act", lambda e, t=t: e.activation(hn[:, t % 2, :], X[:, t, :], AF.Copy, scale=rstd[:, t:t + 1]),
             reads=[("X", t), "rstd"], writes=[("hn", t % 2)])
        if lvl < 4:
            continue
        bank = t % 2
        tp = ps[:, bank, :].bitcast(BF16)
        for cc in range(8):
            P.op("pe", lambda e, t=t, cc=cc, tp=tp: e.transpose(tp[:, cc * 128:(cc + 1) * 128],
                                                             hn[:, t % 2, cc * 128:(cc + 1) * 128], ident),
                 reads=[("hn", t % 2), "ident"], writes=[("ps", bank)])
        if lvl < 5:
            continue
        P.op("dve", lambda e, t=t, tp=tp: e.tensor_copy(hnT[:, :, t * 128:(t + 1) * 128],
                                                        tp.rearrange("p (c k) -> p c k", k=128)),
             reads=[("ps", bank)], writes=["hnT"])

    wv = w_d.ap().rearrange("(c p) n -> p c n", p=128)
    cnt = 0
    def wload(blk):
        slot = blk % 2
        for h2 in range(2):
            P.dma("sp", wst[:, slot * 8 + h2 * 4:slot * 8 + h2 * 4 + 4, :],
                  wv[:, h2 * 4:h2 * 4 + 4, blk * 512:(blk + 1) * 512], writes=[("wst", slot, h2)])

    blks = list(DBG.get("blks", range(nblk)))
    wload(blks[0])
    for bi, blk in enumerate(blks):
        slot = blk % 2
        if bi + 1 < len(blks):
            wload(blks[bi + 1])
        for cc in range(8):
            eng = "act" if cc % 2 == 0 else "pool"
            if eng == "act":
                P.op("act", lambda e, cc=cc, slot=slot: e.activation(wb[:, slot * 8 + cc, :], wst[:, slot * 8 + cc, :],
                                                                   AF.Copy, scale=G[:, cc:cc + 1]),
                     reads=[("wst", slot, cc // 4), "G"], writes=[("wb", slot, cc)])
            else:
                P.op("pool", lambda e, cc=cc, slot=slot: e.tensor_scalar(wb[:, slot * 8 + cc, :], wst[:, slot * 8 + cc, :],
                                                                       G[:, cc:cc + 1], None, ALU.mult),
                     reads=[("wst", slot, cc // 4), "G"], writes=[("wb", slot, cc)])
        if kind == "A":
            sec = blk // 2
            hh = blk % 2
            if sec < 9:
                g, which = sec // 3, sec % 3
            else:
                g, which = 0, 3
        else:
            which = blk // 2
            hh = blk % 2
            g = 0
        r = DIL[g] if kind == "A" else 1
        for tt in range(NTILE):
            bank = 2 + (cnt % 6)
            k4 = cnt % 4
            cnt += 1
            per = NTILE // r
            rr, nl = tt // per, tt % per
            start = rr + r * 128 * nl
            stop = start + r * 127 + 1
            for cc in range(8):
                P.op("pe", lambda e, cc=cc, slot=slot, bank=bank, start=start, stop=stop, r=r:
                     e.matmul(ps[:, bank, :], hnT[:, cc, start:stop:r], wb[:, slot * 8 + cc, :],
                              start=(cc == 0), stop=(cc == 7)),
                     reads=["hnT", ("wb", slot, cc)], writes=[("ps", bank)])
            pv = ps[:, bank, :].rearrange("p (h d) -> p h d", d=hd)
            sv = st[:, k4, :].rearrange("p (h d) -> p h d", d=hd)
            if which in (0, 1):
                ri = (g * NTILE + tt) * 2
                if DBG.get("nobc"):
                    C = rope[:, 0:nh, :]
                    S = rope[:, 0:nh, :]
                else:
                    C = rope[:, ri, :].unsqueeze(1).to_broadcast([128, nh, half])
                    S = rope[:, ri + 1, :].unsqueeze(1).to_broadcast([128, nh, half])
                tv = [tmp[:, k4 * 4 + i, :].rearrange("p (h d) -> p h d", d=half) for i in range(4)]
                x1, x2 = pv[:, :, 0:half], pv[:, :, half:2 * half]
                rl = DBG.get("rope_lvl", 9)
                P.op("act", lambda e, pv=pv, sv=sv: e.activation(sv[:, :, 2 * half:], pv[:, :, 2 * half:], AF.Copy),
                     reads=[("ps", bank)], writes=[("st", k4, "p")])
                ser = [("st", k4, "p")] if DBG.get("ser") else []
                if rl >= 1:
                  P.op("dve", lambda e, tv=tv, x1=x1, C=C: e.tensor_tensor(tv[0], x1, C, ALU.mult),
                     reads=[("ps", bank), "rope"] + ser, writes=[("tmp", k4, 0)])
                if rl >= 2:
                  P.op("dve", lambda e, tv=tv, x2=x2, S=S: e.tensor_tensor(tv[1], x2, S, ALU.mult),
                     reads=[("ps", bank), "rope"], writes=[("tmp", k4, 1)])
                if rl >= 3:
                  P.op("dve", lambda e, tv=tv, x2=x2, C=C: e.tensor_tensor(tv[2], x2, C, ALU.mult),
                     reads=[("ps", bank), "rope"], writes=[("tmp", k4, 2)])
                if rl >= 3:
                  P.op("dve", lambda e, tv=tv, x1=x1, S=S: e.tensor_tensor(tv[3], x1, S, ALU.mult),
                     reads=[("ps", bank), "rope"], writes=[("tmp", k4, 3)])
                if rl >= 4:
                  P.op(DBG.get("rope_eng", "dve"), lambda e, tv=tv, sv=sv: e.tensor_tensor(sv[:, :, 0:half], tv[0], tv[1], ALU.subtract),
                     reads=[("tmp", k4, 0), ("tmp", k4, 1)], writes=[("st", k4, "r1")])
                if rl >= 5:
                  P.op(DBG.get("rope_eng", "dve"), lambda e, tv=tv, sv=sv: e.tensor_tensor(sv[:, :, half:2 * half], tv[2], tv[3], ALU.add),
                     reads=[("tmp", k4, 2), ("tmp", k4, 3)], writes=[("st", k4, "r2")])
                streads = [("st", k4, "r1"), ("st", k4, "r2"), ("st", k4, "p")]
            elif which == 2:
                streads = [("st", k4, "r1"), ("st", k4, "r2"), ("st", k4, "p")]
                P.op("act", lambda e, k4=k4, bank=bank: e.activation(st[:, k4, :], ps[:, bank, :], AF.Copy),
                     reads=[("ps", bank)], writes=streads)
            else:
                streads = [("st", k4, "r1"), ("st", k4, "r2"), ("st", k4, "p")]
                P.op("act", lambda e, k4=k4, bank=bank: e.activation(st[:, k4, :], ps[:, bank, :], AF.Silu),
                     reads=[("ps", bank)], writes=streads)
            if which < 3:
                slotidx = g * 3 + which if kind == "A" else which
                dst = send_d.ap()[slotidx, tt * 128:(tt + 1) * 128, hh * 512:(hh + 1) * 512]
                P.dma(c.q2(), dst, st[:, k4, :], reads=streads, writes=[("send", slotidx, hh, tt)])
            else:
                P.dma(c.q2(), zs_d.ap()[tt * 128:(tt + 1) * 128, hh * 512:(hh + 1) * 512], st[:, k4, :],
                      reads=streads, writes=[("zs", hh, tt)])


def _load_tok(c, dst, src_of_quarter, name):
    for qi in range(4):
        c.P.dma(c.q(), dst[:, qi * 16:(qi + 1) * 16, :], src_of_quarter(qi), writes=[(name, qi)])


def _transpose_tok(c, src, dstT, name, nameT, bank0, dstT2=None):
    P, ps = c.P, c.ps
    for grp in range(8):
        bank = bank0 + grp % 2
        tp = ps[:, bank, :].bitcast(BF16)
        for k in range(8):
            blk = grp * 8 + k
            P.op("pe", lambda e, tp=tp, k=k, blk=blk: e.transpose(tp[:, k * 128:(k + 1) * 128], src[:, blk, 0:128], c.ident),
                 reads=[(name, blk // 16), "ident"], writes=[("ps", bank)])
        if dstT2 is not None:
            P.op("dve", lambda e, tp=tp, grp=grp: e.tensor_copy(dstT[0:64, grp * 1024:(grp + 1) * 1024], tp[0:64, :]),
                 reads=[("ps", bank), "qzero"], writes=[(nameT, grp)])
            P.op("act", lambda e, tp=tp, grp=grp: e.activation(dstT2[64:128, grp * 1024:(grp + 1) * 1024], tp[64:128, :], AF.Copy),
                 reads=[("ps", bank), "qzero"], writes=[(nameT + "2", grp)])
        elif grp % 2 == 0:
            P.op("dve", lambda e, tp=tp, grp=grp: e.tensor_copy(dstT[:, grp * 1024:(grp + 1) * 1024], tp),
                 reads=[("ps", bank)], writes=[(nameT, grp)])
        else:
            P.op("act", lambda e, tp=tp, grp=grp: e.activation(dstT[:, grp * 1024:(grp + 1) * 1024], tp, AF.Copy),
                 reads=[("ps", bank)], writes=[(nameT, grp)])


def _copy_v(c, Vtok, Vaug, vname):
    for hf in range(2):
        c.P.op("pool", lambda e, hf=hf: e.tensor_copy(Vaug[:, hf * 32:(hf + 1) * 32, 0:128], Vtok[:, hf * 32:(hf + 1) * 32, :]),
               reads=[(vname, 2 * hf), (vname, 2 * hf + 1)], writes=[("V", hf)])


def emit_attn_A(c, tok_src, ret_dst, oscr_d, mask_d, nb=2):
    P, ar, ps = c.P, c.ar, c.ps
    c.stage_begin()
    toks = [[ar.alloc([128, 64, 128], BF16) for _ in range(3)] for _ in range(2)]
    Vaug = ar.alloc([128, 64, 130], BF16)
    QT = ar.alloc([128, SEQ], BF16)
    KT = ar.alloc([128, SEQ], BF16)
    mask = ar.alloc([128, 512], BF16)
    PT = ar.alloc([128, 4, 512], BF16)
    osb = ar.alloc([128, 8, 130], F32)
    oc = ar.alloc([128, 2 * 4 * 3, 130], F32)
    osum = ar.alloc([128, 2 * 4, 130], F32)
    rec = ar.alloc([128, 2 * 4], F32)
    ob = ar.alloc([128, 2 * 4, 128], BF16)
    scale = 128.0 ** -0.5
    P.dma("sp", mask, mask_d.ap(), writes=["mask"])
    P.op("pool", lambda e: e.memset(Vaug[:, :, 128:130], 1.0), writes=["Vones"])
    n = 0
    units = [(b, g) for b in range(nb) for g in range(3)]

    def loads(u):
        b_, g_ = units[u]
        for w3 in range(3):
            _load_tok(c, toks[u % 2][w3], lambda qi, w3=w3: tok_src(b_, g_ * 3 + w3, qi), ("tok", u % 2, w3))

    loads(0)
    for u, (b, g) in enumerate(units):
        if True:
            r = DIL[g]
            per = NTILE // r
            if u + 1 < len(units):
                loads(u + 1)
            _copy_v(c, toks[u % 2][2], Vaug, ("tok", u % 2, 2))
            _transpose_tok(c, toks[u % 2][0], QT, ("tok", u % 2, 0), "QT", 0)
            _transpose_tok(c, toks[u % 2][1], KT, ("tok", u % 2, 1), "KT", 0)

            def prevblk(qb):
                il, kbi = qb // 16, qb % 16
                rr, nl = kbi // per, kbi % per
                if nl > 0:
                    return qb - 1
                if il > 0:
                    return (il - 1) * 16 + rr * per + per - 1
                return None

            def emit_S(s, n):
                bank = 2 + n % 3
                for j in range(2):
                    qb = 2 * s + j
                    pb = prevblk(qb)
                    kbs = (qb, qb if pb is None else pb)
                    for w2 in range(2):
                        kb = kbs[w2]
                        c0 = j * 256 + w2 * 128
                        P.op("pe", lambda e, bank=bank, c0=c0, kb=kb, qb=qb: e.matmul(
                            ps[:, bank, c0:c0 + 128], KT[:, kb * 128:(kb + 1) * 128], QT[:, qb * 128:(qb + 1) * 128],
                            start=True, stop=True),
                            reads=[("KT", kb // 8), ("QT", qb // 8)], writes=[("ps", bank)])

            def emit_rest(s, n):
                bank = 2 + n % 3
                k4 = n % 4
                P.op("act", lambda e, bank=bank, k4=k4: e.activation(PT[:, k4, :], ps[:, bank, :], AF.Exp, scale=scale),
                     reads=[("ps", bank)], writes=[("PT", k4), ("PTm", k4)])
                P.op("dve", lambda e, k4=k4: e.tensor_tensor(PT[:, k4, :], PT[:, k4, :], mask, ALU.mult),
                     reads=[("PT", k4), "mask"], writes=[("PTm", k4)])
                for j in range(2):
                    qb = 2 * s + j
                    pb = prevblk(qb)
                    obank = 5 + (2 * n + j) % 3
                    o8 = (2 * n + j) % 8
                    il, kbi = qb // 16, qb % 16
                    rr, nl = kbi // per, kbi % per
                    P.op("pe", lambda e, obank=obank, k4=k4, j=j, qb=qb, pb=pb: e.matmul(
                        ps[:, obank, 0:130], PT[:, k4, j * 256:j * 256 + 128], Vaug[:, qb, :], start=True, stop=(pb is None)),
                        reads=[("PTm", k4), ("V", qb // 32), "Vones"], writes=[("ps", obank)])
                    if pb is not None:
                        P.op("pe", lambda e, obank=obank, k4=k4, j=j, pb=pb: e.matmul(
                            ps[:, obank, 0:130], PT[:, k4, j * 256 + 128:j * 256 + 256], Vaug[:, pb, :], start=False, stop=True),
                            reads=[("PTm", k4), ("V", pb // 32), "Vones"], writes=[("ps", obank)])
                    P.op("dve", lambda e, obank=obank, o8=o8: e.tensor_copy(osb[:, o8, :], ps[:, obank, 0:130]),
                         reads=[("ps", obank)], writes=[("osb", o8)])
                    row0 = b * SEQ + il * NT + rr + r * nl * 128
                    P.dma(c.q2(), oscr_d.ap()[g, row0:row0 + r * 127 + 1:r, 0:130], osb[:, o8, :],
                          reads=[("osb", o8)], writes=[("oscr", b, g, qb)])

            emit_S(0, n)
            emit_S(1, n + 1)
            for s in range(32):
                if s + 2 < 32:
                    emit_S(s + 2, n + 2)
                emit_rest(s, n)
                n += 1
        if g < 2:
            continue
        for Tq in range(16):
            k2 = Tq % 2
            deps = []
            for T in range(4 * Tq, 4 * Tq + 4):
                il, Tl = T // 16, T % 16
                deps.append(("oscr", b, 0, T))
                deps += [("oscr", b, 1, il * 16 + rr * 4 + Tl // 4) for rr in range(4)]
                deps += [("oscr", b, 2, il * 16 + rr) for rr in range(16)]
            deps = sorted(set(deps))
            ocv = oc[:, k2 * 12:(k2 + 1) * 12, :].rearrange("p (j g) c -> p j g c", g=3)
            osv = osum[:, k2 * 4:(k2 + 1) * 4, :]
            for g in range(3):
                P.dma(c.q2(), ocv[:, :, g, :],
                      oscr_d.ap()[g, b * SEQ + Tq * 512:b * SEQ + (Tq + 1) * 512, 0:130].rearrange("(j p) c -> p j c", p=128),
                      reads=deps, writes=[("oc", k2, g)])
            P.op("dve", lambda e, ocv=ocv, osv=osv: e.tensor_tensor(osv, ocv[:, :, 0, :], ocv[:, :, 1, :], ALU.add),
                 reads=[("oc", k2, 0), ("oc", k2, 1)], writes=[("osum", k2), ("osum2", k2)])
            P.op("dve", lambda e, ocv=ocv, osv=osv: e.tensor_tensor(osv, osv, ocv[:, :, 2, :], ALU.add),
                 reads=[("oc", k2, 2), ("osum", k2)], writes=[("osum2", k2)])
            rv = rec[:, k2 * 4:(k2 + 1) * 4].unsqueeze(2)
            P.op("dve", lambda e, rv=rv, osv=osv: e.reciprocal(rv, osv[:, :, 128:129]),
                 reads=[("osum2", k2)], writes=[("rec", k2)])
            obv = ob[:, k2 * 4:(k2 + 1) * 4, :]
            P.op("dve", lambda e, rv=rv, osv=osv, obv=obv: e.tensor_tensor(obv, osv[:, :, 0:128], rv.to_broadcast([128, 4, 128]), ALU.mult),
                 reads=[("osum2", k2), ("rec", k2)], writes=[("ob", k2)])
            P.dma(c.q2(), ret_dst(b, 4 * Tq, 4), obv, reads=[("ob", k2)], writes=[("ret", b, Tq)])


def emit_attn_B(c, tok_src, ret_dst, mask_d, lam_d, subln_d, cst_d, nb=2):
    P, ar, ps = c.P, c.ar, c.ps
    c.stage_begin()
    toks = [[ar.alloc([128, 64, 128], BF16) for _ in range(3)] for _ in range(2)]
    Vaug = ar.alloc([128, 64, 130], BF16)
    QT = ar.alloc([128, SEQ], BF16)
    QT2 = ar.alloc([128, SEQ], BF16)
    KT = ar.alloc([128, SEQ], BF16)
    mask = ar.alloc([128, 512], BF16)
    PT = ar.alloc([128, 2 * 2, 512], BF16)
    lpb = ar.alloc([128, 256], F32)
    lpp = ar.alloc([128, 128], F32)
    lsum = ar.alloc([128, 2], F32)
    lam = ar.alloc([128, 4], F32)
    cst = ar.alloc([128, 2], F32)
    G2 = ar.alloc([128, 128], F32)
    rc = ar.alloc([128, 2 * 4], F32)
    o1 = ar.alloc([128, 2, 128], F32)
    av = ar.alloc([128, 2, 128], F32)
    sqj = ar.alloc([128, 2, 128], F32)
    ss2 = ar.alloc([128, 2], F32)
    rs2 = ar.alloc([128, 4], F32)
    rstd2 = ar.alloc([128, 4], F32)
    avs = ar.alloc([128, 4, 128], F32)
    pending = []
    ob = ar.alloc([128, 2 * 2, 128], BF16)
    scale = 64.0 ** -0.5
    P.dma("sp", mask, mask_d.ap(), writes=["mask"])
    P.dma("sp", lpb, lam_d.ap().rearrange("a b -> (a b)").partition_broadcast(128), writes=["lpb"])
    P.dma("sp", G2, subln_d.ap().partition_broadcast(128), writes=["G2raw"])
    P.dma("sp", cst, cst_d.ap(), writes=["cst"])
    P.op("pool", lambda e: e.memset(Vaug[:, :, 128:130], 1.0), writes=["Vones"])
    P.op("pool", lambda e: e.memset(QT[64:128, :], 0.0), writes=["qzero"])
    P.op("pool", lambda e: e.memset(QT2[0:64, :], 0.0), writes=["qzero"])
    P.op("dve", lambda e: e.tensor_tensor(lpp[:, 0:64], lpb[:, 0:64], lpb[:, 64:128], ALU.mult), reads=["lpb"], writes=["lpp0"])
    P.op("dve", lambda e: e.tensor_tensor(lpp[:, 64:128], lpb[:, 128:192], lpb[:, 192:256], ALU.mult), reads=["lpb"], writes=["lpp1"])
    P.op("dve", lambda e: e.tensor_reduce(lsum[:, 0:2], lpp.rearrange("p (a b) -> p a b", b=64), AX.X, ALU.add),
         reads=["lpp0", "lpp1"], writes=["lsum"])
    P.op("act", lambda e: e.activation(lam[:, 0:2], lsum[:, 0:2], AF.Exp), reads=["lsum"], writes=["lexp"])
    P.op("dve", lambda e: e.tensor_tensor(lam[:, 2:3], lam[:, 0:1], lam[:, 1:2], ALU.subtract), reads=["lexp"], writes=["ldiff"])
    P.op("dve", lambda e: e.tensor_scalar(lam[:, 3:4], lam[:, 2:3], cst[:, 0:1], -1.0, ALU.add, ALU.mult),
         reads=["ldiff", "cst"], writes=["nlam"])
    P.op("dve", lambda e: e.tensor_scalar(G2, G2, cst[:, 1:2], None, ALU.mult), reads=["G2raw", "cst"], writes=["G2"])
    gstep = 0

    def loads(u):
        for w3 in range(3):
            _load_tok(c, toks[u % 2][w3], lambda qi, w3=w3: tok_src(u, w3, qi), ("tok", u % 2, w3))

    loads(0)
    for b in range(nb):
        if b + 1 < nb:
            loads(b + 1)
        _copy_v(c, toks[b % 2][2], Vaug, ("tok", b % 2, 2))
        _transpose_tok(c, toks[b % 2][0], QT, ("tok", b % 2, 0), "QT", 0, dstT2=QT2)
        _transpose_tok(c, toks[b % 2][1], KT, ("tok", b % 2, 1), "KT", 0)
        steps = [(qc, kp) for qc in range(32) for kp in range(qc + 1)]

        def emit_S(i):
            qc, kp = steps[i]
            bufi = (gstep + i) % 2
            q0 = qc * 256
            for kbi in range(2):
                kb = 2 * kp + kbi
                for comp in range(2):
                    bank = bufi * 2 + comp
                    lo, hi = comp * 64, comp * 64 + 64
                    if DBG.get("noS") and i > 3:
                        continue
                    qsrc = QT if comp == 0 else QT2
                    P.op("pe", lambda e, bank=bank, kbi=kbi, kb=kb, qsrc=qsrc, q0=q0:
                         e.matmul(ps[:, bank, kbi * 256:(kbi + 1) * 256], KT[:, kb * 128:(kb + 1) * 128],
                                  qsrc[:, q0:q0 + 256], start=True, stop=True),
                         reads=[("KT", kb // 8), ("QT" if comp == 0 else "QT2", q0 // 1024), "qzero"], writes=[("ps", bank)])

        def emit_rest(i):
            qc, kp = steps[i]
            bufi = (gstep + i) % 2
            diag = (kp == qc)
            P.op("act", lambda e, bufi=bufi: e.activation(PT[:, bufi * 2:bufi * 2 + 2, :], ps[:, bufi * 2:bufi * 2 + 2, :], AF.Exp, scale=scale),
                 reads=[("ps", bufi * 2), ("ps", bufi * 2 + 1)],
                 writes=[("PT", bufi, 0), ("PTa", bufi, 0), ("PT", bufi, 1), ("PTa", bufi, 1)])
            while pending:
                pending.pop(0)()
            for comp in range(2):
                if diag:
                    P.op("dve", lambda e, bufi=bufi, comp=comp: e.tensor_tensor(
                        PT[:, bufi * 2 + comp, :].rearrange("p (a b) -> p a b", b=128)[:, 0:4:3, :],
                        PT[:, bufi * 2 + comp, :].rearrange("p (a b) -> p a b", b=128)[:, 0:4:3, :],
                        mask[:, 0:128].unsqueeze(1).to_broadcast([128, 2, 128]), ALU.mult),
                        reads=[("PT", bufi, comp), "mask"], writes=[("PTa", bufi, comp)])
            for kbi in range(2):
                kb = 2 * kp + kbi
                for j in range(2):
                    if diag and kbi == 1 and j == 0:
                        continue
                    first = (kp == 0 and kbi == 0)
                    last = (diag and kbi == j)
                    for comp in range(2):
                        obank = 4 + j * 2 + comp
                        if DBG.get("noPV") and not (first or last):
                            continue
                        P.op("pe", lambda e, obank=obank, bufi=bufi, comp=comp, kbi=kbi, j=j, kb=kb, first=first, last=last:
                             e.matmul(ps[:, obank, 0:130], PT[:, bufi * 2 + comp, kbi * 256 + j * 128:kbi * 256 + (j + 1) * 128],
                                      Vaug[:, kb, :], start=first, stop=last),
                             reads=[("PTa", bufi, comp), ("V", kb // 32), "Vones"], writes=[("ps", obank)])
            if not diag:
                return
            e2 = qc % 2
            for j in range(2):
                b1, b2 = 4 + j * 2, 4 + j * 2 + 1
                P.op("dve", lambda e, j=j, b1=b1: e.reciprocal(rc[:, j * 4:j * 4 + 1], ps[:, b1, 128:129]),
                     reads=[("ps", b1)], writes=[("rc1", j)])
                P.op("dve", lambda e, j=j, b2=b2: e.reciprocal(rc[:, j * 4 + 1:j * 4 + 2], ps[:, b2, 128:129]),
                     reads=[("ps", b2)], writes=[("rc2", j)])
                P.op("dve", lambda e, j=j: e.tensor_tensor(rc[:, j * 4 + 2:j * 4 + 3], rc[:, j * 4 + 1:j * 4 + 2], lam[:, 3:4], ALU.mult),
                     reads=[("rc2", j), "nlam"], writes=[("rc3", j)])
                P.op("dve", lambda e, j=j, b1=b1: e.tensor_scalar(o1[:, j, :], ps[:, b1, 0:128], rc[:, j * 4:j * 4 + 1], None, ALU.mult),
                     reads=[("ps", b1), ("rc1", j)], writes=[("o1", j)])
                P.op("dve", lambda e, j=j, b2=b2: e.scalar_tensor_tensor(av[:, j, :], ps[:, b2, 0:128], rc[:, j * 4 + 2:j * 4 + 3], o1[:, j, :], ALU.mult, ALU.add),
                     reads=[("ps", b2), ("rc3", j), ("o1", j)], writes=[("av", j)])
                P.op("pool", lambda e, j=j: e.tensor_tensor(sqj[:, j, :], av[:, j, :], av[:, j, :], ALU.mult),
                     reads=[("av", j)], writes=[("sqj", j)])
                P.op("dve", lambda e, j=j: e.tensor_reduce(ss2[:, j:j + 1], sqj[:, j, :], AX.X, ALU.add),
                     reads=[("sqj", j)], writes=[("ss2", j)])
                P.op("dve", lambda e, j=j, e2=e2: e.tensor_scalar(rs2[:, e2 * 2 + j:e2 * 2 + j + 1], ss2[:, j:j + 1], 1.0 / 128.0, EPS, ALU.mult, ALU.add),
                     reads=[("ss2", j)], writes=[("rs2", e2, j)])
                P.op("pool", lambda e, j=j, e2=e2: e.tensor_copy(avs[:, e2 * 2 + j, :], av[:, j, :]),
                     reads=[("av", j)], writes=[("avs", e2, j)])

            def phase_b(qc=qc, e2=e2, b=b):
                P.op("act", lambda e: e.activation(rs2[:, e2 * 2:e2 * 2 + 2], rs2[:, e2 * 2:e2 * 2 + 2], AF.Sqrt),
                     reads=[("rs2", e2, 0), ("rs2", e2, 1)], writes=[("rs2b", e2)])
                P.op("dve", lambda e: e.reciprocal(rstd2[:, e2 * 2:e2 * 2 + 2], rs2[:, e2 * 2:e2 * 2 + 2]),
                     reads=[("rs2b", e2)], writes=[("rstd2", e2)])
                for j in range(2):
                    P.op("dve", lambda e, j=j: e.scalar_tensor_tensor(ob[:, e2 * 2 + j, :], avs[:, e2 * 2 + j, :], rstd2[:, e2 * 2 + j:e2 * 2 + j + 1], G2, ALU.mult, ALU.mult),
                         reads=[("avs", e2, j), ("rstd2", e2), "G2"], writes=[("ob", e2, j)])
                P.dma("sp", ret_dst(b, 2 * qc, 2), ob[:, e2 * 2:e2 * 2 + 2, :],
                      reads=[("ob", e2, 0), ("ob", e2, 1)], writes=[("ret", b, qc)])

            pending.append(phase_b)

        emit_S(0)
        for i in range(len(steps)):
            if i + 1 < len(steps):
                emit_S(i + 1)
            emit_rest(i)
        while pending:
            pending.pop(0)()
        gstep += len(steps)


def emit_out(c, retrecv_d, zs_d, wo_d):
    P, ar, ps = c.P, c.ar, c.ps
    c.stage_begin()
    X = c.X
    wst = ar.alloc([128, 8, DM], F32)
    wo = ar.alloc([128, 8, DM], BF16)
    oc = ar.alloc([128, 2, DM], BF16)
    zt = ar.alloc([128, 2, DM], BF16)
    y = ar.alloc([128, 2, DM], BF16)
    yT = ar.alloc([128, 2 * 8, 128], BF16)
    wv = wo_d.ap().rearrange("(c p) n -> p c n", p=128)
    for h2 in range(4):
        P.dma(c.q(), wst[:, h2 * 2:h2 * 2 + 2, :], wv[:, h2 * 2:h2 * 2 + 2, :], writes=[("wst", h2)])
    for cc in range(8):
        if cc % 2 == 0:
            P.op("act", lambda e, cc=cc: e.activation(wo[:, cc, :], wst[:, cc, :], AF.Copy), reads=[("wst", cc // 2)], writes=[("wo", cc)])
        else:
            P.op("pool", lambda e, cc=cc: e.tensor_copy(wo[:, cc, :], wst[:, cc, :]), reads=[("wst", cc // 2)], writes=[("wo", cc)])
    for t in range(NTILE):
        k2 = t % 2
        P.dma("sp", oc[:, k2, :], retrecv_d.ap()[t * 128:(t + 1) * 128, :], writes=[("oc", k2)])
        P.dma("act", zt[:, k2, :], zs_d.ap()[t * 128:(t + 1) * 128, :], writes=[("zt", k2)])
        P.op("dve", lambda e, k2=k2: e.tensor_tensor(y[:, k2, :], oc[:, k2, :], zt[:, k2, :], ALU.mult),
             reads=[("oc", k2), ("zt", k2)], writes=[("y", k2)])
        bank = k2
        tp = ps[:, bank, :].bitcast(BF16)
        for cc in range(8):
            P.op("pe", lambda e, tp=tp, cc=cc, k2=k2: e.transpose(tp[:, cc * 128:(cc + 1) * 128], y[:, k2, cc * 128:(cc + 1) * 128], c.ident),
                 reads=[("y", k2), "ident"], writes=[("ps", bank)])
        P.op("act", lambda e, tp=tp, k2=k2: e.activation(yT[:, k2 * 8:(k2 + 1) * 8, :], tp.rearrange("p (c k) -> p c k", k=128), AF.Copy),
             reads=[("ps", bank)], writes=[("yT", k2)])
        for hf in range(2):
            bank2 = 2 + (t * 2 + hf) % 4
            for cc in range(8):
                P.op("pe", lambda e, bank2=bank2, cc=cc, k2=k2, hf=hf: e.matmul(ps[:, bank2, :], yT[:, k2 * 8 + cc, :], wo[:, cc, hf * 512:(hf + 1) * 512],
                                                                              start=(cc == 0), stop=(cc == 7)),
                     reads=[("yT", k2), ("wo", cc)], writes=[("ps", bank2)])
            P.op("dve", lambda e, bank2=bank2, t=t, hf=hf: e.tensor_tensor(X[:, t, hf * 512:(hf + 1) * 512], X[:, t, hf * 512:(hf + 1) * 512], ps[:, bank2, :], ALU.add),
                 reads=[("ps", bank2), ("X", t)], writes=[("X", t)])


def emit_store_x(c, xo_d):
    xv = xo_d.ap().rearrange("(t p) f -> p t f", p=128)
    for k in range(4):
        c.P.dma(c.q(), xv[:, 4 * k:4 * k + 4, :], c.X[:, 4 * k:4 * k + 4, :],
                reads=[("X", t) for t in range(4 * k, 4 * k + 4)], writes=[("xo", k)])


def emit_final_norm(c, gf_d, y_d):
    P, ar = c.P, c.ar
    c.stage_begin()
    X = c.X
    GF = ar.alloc([128, DM], F32)
    sq = ar.alloc([128, 2, DM], BF16)
    ss = ar.alloc([128, NTILE], F32)
    rs = ar.alloc([128, NTILE], F32)
    rstd = ar.alloc([128, NTILE], F32)
    yo = ar.alloc([128, 2, DM], F32)
    P.dma("sp", GF, gf_d.ap().partition_broadcast(128), writes=["GF"])
    for t in range(NTILE):
        P.op("act", lambda e, t=t: e.activation(sq[:, t % 2, :], X[:, t, :], AF.Square, accum_out=ss[:, t:t + 1]),
             reads=[("X", t)], writes=[("ss", t), ("sq", t % 2)])
    emit_rstd(c, ss, rs, rstd, NTILE, DM, [("ss", t) for t in range(NTILE)])
    yv = y_d.ap().rearrange("(t p) f -> p t f", p=128)
    for t in range(NTILE):
        k2 = t % 2
        P.op("dve", lambda e, t=t, k2=k2: e.scalar_tensor_tensor(yo[:, k2, :], X[:, t, :], rstd[:, t:t + 1], GF, ALU.mult, ALU.mult),
             reads=[("X", t), "rstd", "GF"], writes=[("yo", k2)])
        P.dma(c.q(), yv[:, t, :], yo[:, k2, :], reads=[("yo", k2)], writes=[("y", t)])


def build_program(stages):
    nc = bass.Bass("TRN2", target_bir_lowering=False)
    c = Ctx(nc, need_x=any(st in ("load_x", "out", "final", "projA", "projB") for st in stages))
    EI, EO = "ExternalInput", "ExternalOutput"
    ident_d = c.dram("ident", [128, 128], BF16, EI)
    emit_consts(c, ident_d)
    for s in stages:
        if s == "load_x":
            emit_load_x(c, c.dram("x_in", [NT, DM], F32, EI))
        elif s == "store_x":
            emit_store_x(c, c.dram("x_out", [NT, DM], F32, EO))
        elif s in ("projA", "projB"):
            kind = s[-1]
            ncols = A_IN if kind == "A" else B_IN
            half = 16 if kind == "A" else 8
            ngr = 3 if kind == "A" else 1
            nslot = 9 if kind == "A" else 3
            emit_proj(c, kind,
                      c.dram("w_in", [DM, ncols], F32, EI),
                      c.dram("g_in", [128, 8], F32, EI),
                      c.dram("rope", [128, ngr * NTILE * 2, half], F32, EI),
                      c.dram("send", [nslot, NT, DM], BF16, EO),
                      c.dram("zs_out", [NT, DM], BF16, EO))
        elif s == "attnA":
            recv_d = c.dram("recv", [2, 9, 128, 64, 128], BF16, EI)
            ret_d = c.dram("ret", [2, 128, 64, 128], BF16, EO)
            emit_attn_A(c, lambda b, slot, qi, r=recv_d: r.ap()[b, slot, :, qi * 16:(qi + 1) * 16, :],
                        lambda b, T0, n, r=ret_d: r.ap()[b, :, T0:T0 + n, :],
                        c.dram("oscr", [3, 2 * SEQ, 132], F32, "Internal"),
                        c.dram("mask", [128, 512], BF16, EI))
        elif s == "attnB":
            recv_d = c.dram("recv", [2, 3, 128, 64, 128], BF16, EI)
            ret_d = c.dram("ret", [2, 128, 64, 128], BF16, EO)
            emit_attn_B(c, lambda b, slot, qi, r=recv_d: r.ap()[b, slot, :, qi * 16:(qi + 1) * 16, :],
                        lambda b, T0, n, r=ret_d: r.ap()[b, :, T0:T0 + n, :],
                        c.dram("mask", [128, 512], BF16, EI),
                        c.dram("lam", [4, 64], F32, EI),
                        c.dram("subln", [128], F32, EI),
                        c.dram("cst", [128, 2], F32, EI))
        elif s == "out":
            emit_out(c, c.dram("retrecv", [NT, DM], BF16, EI),
                     c.dram("zs_in", [NT, DM], BF16, EI),
                     c.dram("w_out", [DM, DM], F32, EI))
        elif s == "final":
            emit_final_norm(c, c.dram("gf", [DM], F32, EI), c.dram("y", [NT, DM], F32, EO))
        else:
            raise ValueError(s)
    c.P.finish()
    return nc


def _rope_table(kind, core):
    pos0 = (core % 4) * NT
    p = np.arange(128)
    if kind == "A":
        rot = 32
        inv = (1.0 / (ROPE_THETA ** (np.arange(0, rot, 2, dtype=np.float32) / rot))).astype(np.float32)
        out = np.zeros((128, 3 * NTILE * 2, 16), np.float32)
        for g, r in enumerate(DIL):
            per = NTILE // r
            for tt in range(NTILE):
                rr, nl = tt // per, tt % per
                pos = (pos0 + rr + r * (nl * 128 + p)).astype(np.float32)
                ang = pos[:, None] * inv[None, :]
                out[:, (g * NTILE + tt) * 2, :] = np.cos(ang)
                out[:, (g * NTILE + tt) * 2 + 1, :] = np.sin(ang)
        return out
    rot = 16
    inv = (1.0 / (ROPE_THETA ** (np.arange(0, rot, 2, dtype=np.float32) / rot))).astype(np.float32)
    out = np.zeros((128, NTILE * 2, 8), np.float32)
    for tt in range(NTILE):
        pos = (pos0 + tt * 128 + p).astype(np.float32)
        ang = pos[:, None] * inv[None, :]
        out[:, tt * 2, :] = np.cos(ang)
        out[:, tt * 2 + 1, :] = np.sin(ang)
    return out


_BF = ml_dtypes.bfloat16
_PROGS = {}


def _prog(stages):
    key = tuple(stages)
    if key not in _PROGS:
        _PROGS[key] = build_program(stages)
    return _PROGS[key]


def _run(stages, in_maps):
    nc = _prog(stages)
    res = run_bass_kernel_spmd(nc, in_maps, core_ids=list(range(NCORES)))
    return res.results


def _to_heads(send, nslot):
    S = np.stack(send).reshape(2, 4, nslot, 16, 128, 8, 128)
    S = S.transpose(5, 0, 2, 4, 1, 3, 6)
    return [np.ascontiguousarray(S[h]).reshape(2, nslot, 128, 64, 128) for h in range(NCORES)]


def _to_tokens(ret):
    R = np.stack(ret).reshape(8, 2, 128, 4, 16, 128)
    R = R.transpose(1, 3, 4, 2, 0, 5)
    R = np.ascontiguousarray(R).reshape(NCORES, NT, DM)
    return [R[i] for i in range(NCORES)]


def _glayout(g):
    return np.ascontiguousarray(g.reshape(8, 128).T)


def kernel_unfused(x, a_norm, a_w_in, a_w_out, b_norm, b_w_in, b_lambda, b_subln, b_w_out, final_norm):
    import math
    x = np.asarray(x, np.float32)
    xs = [np.ascontiguousarray(s) for s in np.split(x.reshape(-1, DM), NCORES, axis=0)]
    ident = np.eye(128, dtype=np.float32).astype(_BF)
    kk = np.arange(128)[:, None]
    qq = np.arange(128)[None, :]
    mask = np.concatenate([(kk <= qq), (kk >= qq)] * 2, axis=1).astype(np.float32).astype(_BF)
    ropeA = [_rope_table("A", i) for i in range(NCORES)]
    ropeB = [_rope_table("B", i) for i in range(NCORES)]
    w_in = [np.asarray(a_w_in[0]), np.asarray(b_w_in[0]), np.asarray(a_w_in[1]), np.asarray(b_w_in[1])]
    w_out = [np.asarray(a_w_out[0]), np.asarray(b_w_out[0]), np.asarray(a_w_out[1]), np.asarray(b_w_out[1])]
    gs = [_glayout(np.asarray(a_norm[0])), _glayout(np.asarray(b_norm[0])),
          _glayout(np.asarray(a_norm[1])), _glayout(np.asarray(b_norm[1]))]
    kinds = ["A", "B", "A", "B"]

    def proj_inputs(layer, i):
        kd = kinds[layer]
        return {"w_in": np.ascontiguousarray(w_in[layer], dtype=np.float32), "g_in": gs[layer],
                "rope": ropeA[i] if kd == "A" else ropeB[i]}

    res = _run(["load_x", "projA"], [dict(ident=ident, x_in=xs[i], **proj_inputs(0, i)) for i in range(NCORES)])
    send = [r["send"] for r in res]
    zs = [r["zs_out"] for r in res]
    for layer in range(4):
        kd = kinds[layer]
        recv = _to_heads(send, 9 if kd == "A" else 3)
        if kd == "A":
            res = _run(["attnA"], [dict(ident=ident, recv=recv[h], mask=mask) for h in range(NCORES)])
        else:
            j = layer // 2
            lam_init = 0.8 - 0.6 * math.exp(-0.3 * layer)
            cst = np.tile(np.array([[lam_init, 1.0 - lam_init]], np.float32), (128, 1))
            res = _run(["attnB"], [dict(ident=ident, recv=recv[h], mask=mask,
                                       lam=np.ascontiguousarray(b_lambda[j], dtype=np.float32),
                                       subln=np.ascontiguousarray(b_subln[j], dtype=np.float32), cst=cst)
                                  for h in range(NCORES)])
        retrecv = _to_tokens([r["ret"] for r in res])
        wo = np.ascontiguousarray(w_out[layer], dtype=np.float32)
        if layer < 3:
            nk = kinds[layer + 1]
            res = _run(["load_x", "out", "proj" + nk, "store_x"],
                       [dict(ident=ident, x_in=xs[i], retrecv=retrecv[i], zs_in=zs[i], w_out=wo,
                             **proj_inputs(layer + 1, i)) for i in range(NCORES)])
            xs = [r["x_out"] for r in res]
            send = [r["send"] for r in res]
            zs = [r["zs_out"] for r in res]
        else:
            res = _run(["load_x", "out", "final"],
                       [dict(ident=ident, x_in=xs[i], retrecv=retrecv[i], zs_in=zs[i], w_out=wo,
                             gf=np.ascontiguousarray(final_norm, dtype=np.float32)) for i in range(NCORES)])
            ys = [r["y"] for r in res]
    return np.concatenate(ys, axis=0).reshape(2, SEQ, DM).astype(np.float32)


class _V:
    def __init__(self, ap):
        self._ap = ap

    def ap(self):
        return self._ap


def _x_scope_begin(c):
    c.P.barrier()
    c.base = c.base0
    c.ar.reset(c.base0)
    c.X = c.ar.alloc([128, NTILE, DM], F32)
    c.base = c.ar.mark()


def _x_scope_end(c):
    c.base = c.base0
    c.X = None


def build_fused():
    nc = bass.Bass("TRN2", target_bir_lowering=False)
    c = Ctx(nc, need_x=False)
    c.base0 = c.base
    EI, EO, IN = "ExternalInput", "ExternalOutput", "Internal"
    kinds = ["A", "B", "A", "B"]
    ident_d = c.dram("ident", [128, 128], BF16, EI)
    x_in = c.dram("x_in", [4, NT, DM], F32, EI)
    y_d = c.dram("y", [4, NT, DM], F32, EO)
    w_in = [c.dram("w_in%d" % l, [DM, A_IN if kinds[l] == "A" else B_IN], F32, EI) for l in range(4)]
    g_in = [c.dram("g_in%d" % l, [128, 8], F32, EI) for l in range(4)]
    w_out = [c.dram("w_out%d" % l, [DM, DM], F32, EI) for l in range(4)]
    ropeA = c.dram("ropeA", [4, 128, 3 * NTILE * 2, 16], F32, EI)
    ropeB = c.dram("ropeB", [4, 128, NTILE * 2, 8], F32, EI)
    mask_d = c.dram("mask", [128, 512], BF16, EI)
    lam_d = [c.dram("lam%d" % j, [4, 64], F32, EI) for j in range(2)]
    subln_d = [c.dram("subln%d" % j, [128], F32, EI) for j in range(2)]
    cst_d = [c.dram("cst%d" % j, [128, 2], F32, EI) for j in range(2)]
    gf_d = c.dram("gf", [DM], F32, EI)
    xs = c.dram("xs", [4, NT, DM], F32, IN)
    send = c.dram("send", [4, 9, NT, DM], BF16, IN)
    zs = c.dram("zs", [4, NT, DM], BF16, IN)
    retr = c.dram("retr", [4, NT, DM], BF16, IN)
    oscr = c.dram("oscr", [3, SEQ, 132], F32, IN)
    emit_consts(c, ident_d)

    def proj(layer, il):
        kd = kinds[layer]
        rope = ropeA if kd == "A" else ropeB
        emit_proj(c, kd, w_in[layer], g_in[layer], _V(rope.ap()[il]), _V(send.ap()[il]), _V(zs.ap()[il]))

    for il in range(4):
        _x_scope_begin(c)
        emit_load_x(c, _V(x_in.ap()[il]))
        proj(0, il)
    for layer in range(4):
        _x_scope_end(c)
        for h in range(8):
            def tok_src(b, slot, qi, h=h):
                return send.ap()[qi, slot, :, h * 128:(h + 1) * 128].rearrange("(k p) d -> p k d", p=128)

            def ret_dst(b, T0, n, h=h):
                il, Tl = T0 // 16, T0 % 16
                return retr.ap()[il, Tl * 128:(Tl + n) * 128, h * 128:(h + 1) * 128].rearrange("(j p) d -> p j d", p=128)

            if kinds[layer] == "A":
                emit_attn_A(c, tok_src, ret_dst, oscr, mask_d, nb=1)
            else:
                j = layer // 2
                emit_attn_B(c, tok_src, ret_dst, mask_d, lam_d[j], subln_d[j], cst_d[j], nb=1)
        for il in range(4):
            _x_scope_begin(c)
            emit_load_x(c, _V(x_in.ap()[il] if layer == 0 else xs.ap()[il]))
            emit_out(c, _V(retr.ap()[il]), _V(zs.ap()[il]), w_out[layer])
            if layer < 3:
                proj(layer + 1, il)
                emit_store_x(c, _V(xs.ap()[il]))
            else:
                emit_final_norm(c, gf_d, _V(y_d.ap()[il]))
    c.P.finish()
    return nc


def kernel_fused(x, a_norm, a_w_in, a_w_out, b_norm, b_w_in, b_lambda, b_subln, b_w_out, final_norm):
    import math
    x = np.asarray(x, np.float32)
    ident = np.eye(128, dtype=np.float32).astype(_BF)
    kk = np.arange(128)[:, None]
    qq = np.arange(128)[None, :]
    mask = np.concatenate([(kk <= qq), (kk >= qq)] * 2, axis=1).astype(np.float32).astype(_BF)
    ropeA = np.stack([_rope_table("A", i) for i in range(4)])
    ropeB = np.stack([_rope_table("B", i) for i in range(4)])
    f32 = lambda a: np.ascontiguousarray(np.asarray(a), dtype=np.float32)
    common = dict(ident=ident, mask=mask, ropeA=ropeA, ropeB=ropeB, gf=f32(final_norm),
                  w_in0=f32(a_w_in[0]), w_in1=f32(b_w_in[0]), w_in2=f32(a_w_in[1]), w_in3=f32(b_w_in[1]),
                  w_out0=f32(a_w_out[0]), w_out1=f32(b_w_out[0]), w_out2=f32(a_w_out[1]), w_out3=f32(b_w_out[1]),
                  g_in0=_glayout(np.asarray(a_norm[0])), g_in1=_glayout(np.asarray(b_norm[0])),
                  g_in2=_glayout(np.asarray(a_norm[1])), g_in3=_glayout(np.asarray(b_norm[1])),
                  lam0=f32(b_lambda[0]), lam1=f32(b_lambda[1]), subln0=f32(b_subln[0]), subln1=f32(b_subln[1]))
    for j, layer in enumerate((1, 3)):
        lam_init = 0.8 - 0.6 * math.exp(-0.3 * layer)
        common["cst%d" % j] = np.tile(np.array([[lam_init, 1.0 - lam_init]], np.float32), (128, 1))
    in_maps = [dict(common, x_in=np.ascontiguousarray(x[c % 2].reshape(4, NT, DM))) for c in range(NCORES)]
    if "fused" not in _PROGS:
        _PROGS["fused"] = build_fused()
    res = run_bass_kernel_spmd(_PROGS["fused"], in_maps, core_ids=list(range(NCORES))).results
    return np.stack([res[0]["y"].reshape(SEQ, DM), res[1]["y"].reshape(SEQ, DM)]).astype(np.float32)


MODE = "unfused"


def kernel(**inputs):
    if MODE == "fused":
        return kernel_fused(**inputs)
    return kernel_unfused(**inputs)
```
